# Optimizing a Trainium2 kernel written in Bass

```python
import jax, jax.numpy as jnp
from jax import lax
import numpy as np

D_MODEL = 1024
BATCH = 8
SEQ = 2048
DEPTH = 2
DEC_BATCH = 8
DEC_SEQ = 16
PAST_LEN = 2048

CHUNK = 64
HEAD_DIM = 64
N_HEADS = D_MODEL // HEAD_DIM
N_HEADS_SB = N_HEADS // 2
N_HEADS_CB = N_HEADS - N_HEADS_SB
W_SB = N_HEADS_SB * HEAD_DIM
W_CB = N_HEADS_CB * HEAD_DIM
MIX_WIDTH = W_SB + W_CB
BAND_CHUNKS = 8
BAND = BAND_CHUNKS * CHUNK
REL_MAX = 128
D_FF = -(-8 * D_MODEL // (3 * 256)) * 256
PLE_DIM = 256
QBLOCK = 128
EPS = 1e-6

kernel_name = "hymba_stickbreak_chunkband_stream_step"


def rmsnorm(x, g):
    xf = x.astype(jnp.float32)
    y = xf * lax.rsqrt(jnp.mean(xf * xf, axis=-1, keepdims=True) + EPS)
    return y.astype(x.dtype) * g


def split_heads(h, w_in):
    qkv = h @ w_in
    cuts = [W_SB, 2 * W_SB, 3 * W_SB, 3 * W_SB + W_CB, 3 * W_SB + 2 * W_CB]
    parts = jnp.split(qkv, cuts, axis=-1)
    b, t = h.shape[0], h.shape[1]
    heads = [N_HEADS_SB] * 3 + [N_HEADS_CB] * 3
    return [pt.reshape(b, t, n, HEAD_DIM) for pt, n in zip(parts, heads)]


def stick_breaking(q, k, v, q_pos, k_pos):
    z = jnp.einsum('bqhd,bkhd->bhqk', q, k).astype(jnp.float32) * (HEAD_DIM ** -0.5)
    causal = k_pos[None, :] < q_pos[:, None]
    log_keep = jnp.where(causal, jax.nn.log_sigmoid(-z), 0.0)
    between = lax.cumsum(log_keep, axis=3, reverse=True) - log_keep
    w = jnp.where(causal, jnp.exp(jax.nn.log_sigmoid(z) + between), 0.0)
    return jnp.einsum('bhqk,bkhd->bqhd', w.astype(v.dtype), v)


def band_attention(q, k, v, dist, rel_table, valid=None):
    idx = jnp.clip(dist, -REL_MAX, REL_MAX) + REL_MAX
    bias = jnp.transpose(rel_table[idx], (2, 0, 1)).astype(jnp.float32)
    s = jnp.einsum('...qhd,...khd->...hqk', q, k).astype(jnp.float32) * (HEAD_DIM ** -0.5) + bias
    if valid is not None:
        s = jnp.where(valid, s, -jnp.inf)
    p = jax.nn.softmax(s, axis=-1)
    return jnp.einsum('...hqk,...khd->...qhd', p.astype(v.dtype), v)


def chunk_band_prompt(q, k, v, rel_table):
    b, s, h, d = q.shape
    nc = s // CHUNK
    kb_len = (BAND_CHUNKS + 1) * CHUNK
    qc = q.reshape(b, nc, CHUNK, h, d)

    def band(x):
        xc = x.reshape(b, nc, CHUNK, h, d)
        xp = jnp.pad(xc, ((0, 0), (BAND_CHUNKS, 0), (0, 0), (0, 0), (0, 0)))
        return jnp.concatenate([xp[:, i:i + nc] for i in range(BAND_CHUNKS + 1)], axis=2)

    kb, vb = band(k), band(v)
    dist = BAND + jnp.arange(CHUNK)[:, None] - jnp.arange(kb_len)[None, :]
    key_pos = (jnp.arange(nc)[:, None] - BAND_CHUNKS) * CHUNK + jnp.arange(kb_len)[None, :]
    valid = (key_pos >= 0)[:, None, None, :]
    out = band_attention(qc, kb, vb, dist, rel_table, valid)
    return out.reshape(b, s, h, d)


def trunk_layer(x, p, g_mix, w_in, rel_table, g_out_sb, g_out_cb, w_out, g_ffn, w_gate, w_up, w_down,
                g_ple, w_ple_gate, w_ple_proj, past_sb_k=None, past_sb_v=None, past_cb_k=None, past_cb_v=None):
    b, t = x.shape[0], x.shape[1]
    h = rmsnorm(x, g_mix)
    q_a, k_a, v_a, q_b, k_b, v_b = split_heads(h, w_in)
    if past_sb_k is None:
        nb = t // QBLOCK
        k_pos = jnp.arange(t)
        qb = jnp.moveaxis(q_a.reshape(b, nb, QBLOCK, N_HEADS_SB, HEAD_DIM), 1, 0)
        a = lax.map(lambda args: stick_breaking(args[0], k_a, v_a, args[1] * QBLOCK + jnp.arange(QBLOCK), k_pos),
                    (qb, jnp.arange(nb)))
        a = jnp.moveaxis(a, 0, 1).reshape(b, t, N_HEADS_SB, HEAD_DIM)
        o_b = chunk_band_prompt(q_b, k_b, v_b, rel_table)
        new = (k_a, v_a, k_b[:, -BAND:], v_b[:, -BAND:])
    else:
        past = past_sb_k.shape[1]
        q_pos = past + jnp.arange(t)
        ka = jnp.concatenate([past_sb_k, k_a.astype(past_sb_k.dtype)], axis=1)
        va = jnp.concatenate([past_sb_v, v_a.astype(past_sb_v.dtype)], axis=1)
        a = stick_breaking(q_a, ka, va, q_pos, jnp.arange(past + t))
        lb = past_cb_k.shape[1]
        kc = jnp.concatenate([past_cb_k, k_b.astype(past_cb_k.dtype)], axis=1)
        vc = jnp.concatenate([past_cb_v, v_b.astype(past_cb_v.dtype)], axis=1)
        kc_pos = jnp.concatenate([past - lb + jnp.arange(lb), q_pos])
        o_b = band_attention(q_b, kc, vc, q_pos[:, None] - kc_pos[None, :], rel_table)
        new = (k_a, v_a, k_b, v_b)
    a = rmsnorm(a, g_out_sb.reshape(N_HEADS_SB, HEAD_DIM))
    o_b = rmsnorm(o_b, g_out_cb.reshape(N_HEADS_CB, HEAD_DIM))
    mix = jnp.concatenate([a.reshape(b, t, W_SB), o_b.reshape(b, t, W_CB)], axis=-1)
    x = x + mix @ w_out
    h = rmsnorm(x, g_ffn)
    x = x + (jax.nn.silu(h @ w_gate) * (h @ w_up)) @ w_down
    gate = jax.nn.sigmoid(rmsnorm(x, g_ple) @ w_ple_gate)
    x = x + gate * (p @ w_ple_proj)
    return x, new


def setup_inputs(seed: int = 0) -> dict:
    key = jax.random.key(seed)
    ks = jax.random.split(key, 24)
    n = jax.random.normal
    f = jnp.float32
    lb = min(BAND, PAST_LEN)
    return {
        "x_prompt": n(ks[0], (BATCH, SEQ, D_MODEL), f),
        "x_sample": n(ks[1], (DEC_BATCH, DEC_SEQ, D_MODEL), f),
        "p_prompt": n(ks[2], (DEPTH, BATCH, SEQ, PLE_DIM), f),
        "p_sample": n(ks[3], (DEPTH, DEC_BATCH, DEC_SEQ, PLE_DIM), f),
        "cache_sb_k": n(ks[4], (DEPTH, DEC_BATCH, PAST_LEN, N_HEADS_SB, HEAD_DIM), f),
        "cache_sb_v": n(ks[5], (DEPTH, DEC_BATCH, PAST_LEN, N_HEADS_SB, HEAD_DIM), f),
        "cache_cb_k": n(ks[6], (DEPTH, DEC_BATCH, lb, N_HEADS_CB, HEAD_DIM), f),
        "cache_cb_v": n(ks[7], (DEPTH, DEC_BATCH, lb, N_HEADS_CB, HEAD_DIM), f),
        "g_mix": 1.0 + 0.05 * n(ks[8], (DEPTH, D_MODEL), f),
        "w_in": n(ks[9], (DEPTH, D_MODEL, 3 * MIX_WIDTH), f) * D_MODEL ** -0.5,
        "rel_table": 0.5 * n(ks[10], (DEPTH, 2 * REL_MAX + 1, N_HEADS_CB), f),
        "g_out_sb": 1.0 + 0.05 * n(ks[11], (DEPTH, W_SB), f),
        "g_out_cb": 1.0 + 0.05 * n(ks[12], (DEPTH, W_CB), f),
        "w_out": n(ks[13], (DEPTH, MIX_WIDTH, D_MODEL), f) * MIX_WIDTH ** -0.5,
        "g_ffn": 1.0 + 0.05 * n(ks[14], (DEPTH, D_MODEL), f),
        "w_gate": n(ks[15], (DEPTH, D_MODEL, D_FF), f) * D_MODEL ** -0.5,
        "w_up": n(ks[16], (DEPTH, D_MODEL, D_FF), f) * D_MODEL ** -0.5,
        "w_down": n(ks[17], (DEPTH, D_FF, D_MODEL), f) * D_FF ** -0.5,
        "g_ple": 1.0 + 0.05 * n(ks[18], (DEPTH, D_MODEL), f),
        "w_ple_gate": n(ks[19], (DEPTH, D_MODEL, D_MODEL), f) * D_MODEL ** -0.5,
        "w_ple_proj": n(ks[20], (DEPTH, PLE_DIM, D_MODEL), f) * PLE_DIM ** -0.5,
        "g_final": 1.0 + 0.05 * n(ks[21], (D_MODEL,), f),
    }


def reference(x_prompt, x_sample, p_prompt, p_sample, cache_sb_k, cache_sb_v, cache_cb_k, cache_cb_v,
              g_mix, w_in, rel_table, g_out_sb, g_out_cb, w_out, g_ffn, w_gate, w_up, w_down,
              g_ple, w_ple_gate, w_ple_proj, g_final):
    xp, xs = x_prompt, x_sample
    new_p, new_s = [], []
    for i in range(DEPTH):
        w = (g_mix[i], w_in[i], rel_table[i], g_out_sb[i], g_out_cb[i], w_out[i], g_ffn[i],
             w_gate[i], w_up[i], w_down[i], g_ple[i], w_ple_gate[i], w_ple_proj[i])
        xp, np_i = trunk_layer(xp, p_prompt[i], *w)
        xs, ns_i = trunk_layer(xs, p_sample[i], *w, cache_sb_k[i], cache_sb_v[i], cache_cb_k[i], cache_cb_v[i])
        new_p.append(np_i)
        new_s.append(ns_i)
    y_prompt = rmsnorm(xp, g_final)
    y_sample = rmsnorm(xs, g_final)
    return (y_prompt, y_sample,
            jnp.stack([e[0] for e in new_p]), jnp.stack([e[1] for e in new_p]),
            jnp.stack([e[2] for e in new_p]), jnp.stack([e[3] for e in new_p]),
            jnp.stack([e[0] for e in new_s]), jnp.stack([e[1] for e in new_s]),
            jnp.stack([e[2] for e in new_s]), jnp.stack([e[3] for e in new_s]))
```

```python
import numpy as np
import concourse.bass as bass
import concourse.mybir as mybir
from concourse.bass_utils import run_bass_kernel_spmd
from contextlib import ExitStack

F32 = mybir.dt.float32
BF16 = mybir.dt.bfloat16
AF = mybir.ActivationFunctionType
ALU = mybir.AluOpType

T = 2064
TT = [(0, 512), (512, 512), (1024, 512), (1536, 512), (2048, 16)]
TOK128 = [(i * 128, 128) for i in range(16)] + [(2048, 16)]
EPS = 1e-6
NFF = 22


import types


def freeze(fn, depth=0):
    if not isinstance(fn, types.FunctionType) or fn.__closure__ is None or depth > 4:
        return fn
    cells = []
    for c in fn.__closure__:
        try:
            v = c.cell_contents
        except ValueError:
            cells.append(c)
            continue
        if isinstance(v, types.FunctionType) and v.__closure__ is not None:
            v = freeze(v, depth + 1)
        cells.append(types.CellType(v))
    g = types.FunctionType(fn.__code__, fn.__globals__, fn.__name__, fn.__defaults__, tuple(cells))
    g.__kwdefaults__ = fn.__kwdefaults__
    return g


class Tok:
    __slots__ = ("lw", "rd", "name", "excl", "lw_read")

    def __init__(self, name="", dead=None, excl=False):
        self.lw = None
        self.rd = list(dead) if dead else []
        self.name = name
        self.excl = excl
        self.lw_read = False

    def handles(self):
        return ([self.lw] if self.lw is not None else []) + list(self.rd)


class DSem:
    def __init__(self, sem):
        self.sem = sem
        self.cnt = 0


class Sched:
    ENG = ["pe", "act", "dve", "pool", "sp"]
    LIM = 12000

    def __init__(self, nc, es):
        self.nc = nc
        self.es = es
        self.q = {e: [] for e in self.ENG}
        self.csem = {}
        self.ccnt = {}
        self.nsem = 0
        for e in ["pe", "act", "dve", "pool"]:
            self._newsem(e)
        self.waited = {e: {} for e in self.ENG}
        self.semobj = {}
        self.bg = None
        self._in_bg = False
        self.bg2 = None

    def _newsem(self, e):
        self.nsem += 1
        self.csem[e] = self.es.enter_context(self.nc.semaphore("c%s%d" % (e, self.nsem)))
        self.ccnt[e] = 0

    def dsem(self, name):
        self.nsem += 1
        return DSem(self.es.enter_context(self.nc.semaphore(name)))

    def op(self, eng, fn, reads=(), writes=(), dsem=None):
        fn = freeze(fn)
        is_dma = dsem is not None
        waits = {}

        def need(h, kind):
            sem, val, heng = h
            if (not is_dma) and heng == eng and kind != "raw":
                return
            key = id(sem)
            if key not in waits or waits[key][1] < val:
                waits[key] = (sem, val)

        for t in reads:
            if t.lw is not None:
                need(t.lw, "waw" if t.lw_read else "raw")
        for t in writes:
            if t.lw is not None:
                need(t.lw, "waw")
            for h in t.rd:
                need(h, "war")
        wl = []
        wd = self.waited[eng]
        for key, (sem, val) in waits.items():
            if wd.get(key, 0) >= val:
                continue
            wd[key] = val
            wl.append((sem, val))
        if is_dma:
            dsem.cnt += 16
            h = (dsem.sem, dsem.cnt, None)
            dsm = dsem.sem

            def emit(e, wl=wl, fn=fn, dsm=dsm):
                for s, val in wl:
                    e.wait_ge(s, val)
                fn(e).then_inc(dsm, 16)
        else:
            if self.ccnt[eng] >= self.LIM:
                self._newsem(eng)
            self.ccnt[eng] += 1
            sem = self.csem[eng]
            h = (sem, self.ccnt[eng], eng)

            def emit(e, wl=wl, fn=fn, sem=sem):
                for s, val in wl:
                    e.wait_ge(s, val)
                fn(e).then_inc(sem, 1)

        self.q[eng].append(emit)
        self._post = True
        for t in reads:
            if t.excl:
                t.lw = h
                t.lw_read = True
                continue
            if h[2] is not None:
                t.rd = [x for x in t.rd if not (x[2] == h[2] and x[0] is h[0])]
            t.rd.append(h)
        for t in writes:
            t.lw = h
            t.lw_read = False
            t.rd = []
        if self.bg is not None and not self._in_bg:
            self._step_bg()
        return h

    def _step_bg(self):
        self._in_bg = True
        try:
            next(self.bg)
        except StopIteration:
            self.bg = None
        self._in_bg = False
        self.bg2 = None

    def drain_bg(self):
        while self.bg is not None:
            self._step_bg()

    def tick(self):
        if self.bg2 is None:
            return False
        sv, self._in_bg = self._in_bg, True
        try:
            next(self.bg2)
            ok = True
        except StopIteration:
            self.bg2 = None
            ok = False
        self._in_bg = sv
        return ok

    def drain_bg2(self):
        while self.tick():
            pass

    def final_wait(self, eng, handles):
        best = {}
        for (sem, val, _) in handles:
            k = id(sem)
            if k not in best or best[k][1] < val:
                best[k] = (sem, val)
        wl = list(best.values())

        def emit(e, wl=wl):
            for sem, val in wl:
                e.wait_ge(sem, val)

        self.q[eng].append(emit)

    def run(self):
        nc = self.nc
        q = self.q
        with nc.Block() as block:
            @block.tensor
            def _(e):
                for f in q["pe"]:
                    f(e)

            @block.scalar
            def _(e):
                for f in q["act"]:
                    f(e)

            @block.vector
            def _(e):
                for f in q["dve"]:
                    f(e)

            @block.gpsimd
            def _(e):
                for f in q["pool"]:
                    f(e)

            @block.sync
            def _(e):
                for f in q["sp"]:
                    f(e)


class Arena:
    def __init__(self, tens, n):
        self.t = tens
        self.n = n
        self.off = 0
        self.live = []
        self.dead = []

    def reset(self):
        hs = []
        for t in self.live:
            hs.extend(t.handles())
        hs.extend(self.dead)
        best = {}
        for h in hs:
            k = id(h[0])
            if k not in best or best[k][1] < h[1]:
                best[k] = h
        self.dead = list(best.values())
        self.live = []
        self.off = 0

    def alloc(self, n, ntok=1):
        n_al = (n + 15) // 16 * 16
        assert self.off + n_al <= self.n, ("arena overflow", self.off, n_al, self.n)
        ap = self.t[:, self.off:self.off + n]
        self.off += n_al
        toks = [Tok(dead=self.dead) for _ in range(ntok)]
        self.live.extend(toks)
        return ap, (toks[0] if ntok == 1 else toks)


import os
STOP = int(os.environ.get("KSTOP", "0"))


class StopBuild(Exception):
    pass


def stop_at(k):
    if STOP == k:
        raise StopBuild()


NCF = 128 + 640
NCB = 4 * 128 + 4 * 512 + 16 + 256


def build():
    nc = bass.Bass("TRN2", target_bir_lowering=False)

    def din(name, shape):
        return nc.dram_tensor(name, list(shape), F32, kind="ExternalInput").ap()

    def dout(name, shape):
        return nc.dram_tensor(name, list(shape), F32, kind="ExternalOutput").ap()

    xin = din("xin", [T, 1024])
    pin = din("pin", [2, T, 256])
    csk = din("csk", [2, 2048, 512])
    csv = din("csv", [2, 2048, 512])
    cck = din("cck", [2, 512, 512])
    ccv = din("ccv", [2, 512, 512])
    w_in = din("w_in", [2, 1024, 3072])
    w_out = din("w_out", [2, 1024, 1024])
    w_gate = din("w_gate", [2, 1024, 2816])
    w_up = din("w_up", [2, 1024, 2816])
    w_down = din("w_down", [2, 2816, 1024])
    w_pg = din("w_pg", [2, 1024, 1024])
    w_pp = din("w_pp", [2, 256, 1024])
    biasP = din("biasP", [2, 8, 128, 640])
    biasS = din("biasS", [2, 4, 128, 160])
    gpack = din("gpack", [128, 72])
    constF = din("constF", [128, NCF])
    constB = din("constB", [128, NCB])
    y = dout("y", [T, 1024])
    sbk = dout("sbk", [2, T, 512])
    sbv = dout("sbv", [2, T, 512])
    cbk = dout("cbk", [2, 528, 512])
    cbv = dout("cbv", [2, 528, 512])

    with ExitStack() as es:
        S = Sched(nc, es)

        def sb(name, shape, dt):
            return es.enter_context(nc.sbuf_tensor(name, shape, dt))

        xT = sb("xT", [128, 8, T], F32)
        hT = sb("hT", [128, 8, T], BF16)
        X = [[Tok() for _ in TT] for _ in range(8)]
        H = [[Tok() for _ in TT] for _ in range(8)]
        cF = sb("cF", [128, NCF], F32)
        cB = sb("cB", [128, NCB], BF16)
        gp = sb("gp", [128, 72], F32)
        TC = Tok()
        ident = cF[:, 0:128]
        maskc = cF[:, 128:768]
        negTri = cB[:, 0:128]
        negOnes = cB[:, 128:256]
        ones = cB[:, 256:384]
        blk64 = cB[:, 384:512]
        Mr = [cB[:, 512 + r * 512:512 + (r + 1) * 512] for r in range(4)]
        M16 = cB[0:16, 2560:2576]
        identB = cB[:, 2576:2704]
        NEGM = cB[:, 2704:2832]
        NA_B = 34208
        NA_F = 6304
        arB_t = sb("arB", [128, NA_B], BF16)
        arF_t = sb("arF", [128, NA_F], F32)
        arB = Arena(arB_t, NA_B)
        arF = Arena(arF_t, NA_F)
        sq_t = sb("sq", [128, 2, 512], BF16)
        SQ = [Tok(), Tok()]
        rstd_t = sb("rstd", [128, 2, 512], F32)
        RSTD = [Tok(), Tok()]
        stg_t = sb("stg", [128, 3, 256], F32)
        STG = [Tok(), Tok(), Tok()]
        stg_ds = [S.dsem("stg%d" % i) for i in range(3)]
        stg_i = [0]
        pp = [es.enter_context(nc.psum_tensor("pp%d" % i, [128, 1024], F32)) for i in range(4)]
        PB = [Tok(excl=True) for _ in range(8)]

        def bank(i):
            return pp[i // 2][:, (i % 2) * 512:(i % 2) * 512 + 512]

        bank_rr = [0]

        def nextbank(lo=0, hi=8):
            b = lo + bank_rr[0] % (hi - lo)
            bank_rr[0] += 1
            return b

        ld_ds = [S.dsem("ld%d" % i) for i in range(2)]
        ld_i = [0]

        def load(eng, out_ap, in_ap, wtoks, ds=None):
            if ds is None:
                ds = ld_ds[ld_i[0] % len(ld_ds)]
                ld_i[0] += 1
            S.op(eng, lambda e, o=out_ap, i=in_ap: e.dma_start(out=o, in_=i), writes=wtoks, dsem=ds)

        out_handles = []

        def store(out_ap, in_ap, rtoks, ds):
            h = S.op("sp", lambda e, o=out_ap, i=in_ap: e.dma_start(out=o, in_=i), reads=rtoks, dsem=ds)
            out_handles.append(h)

        cds = [S.dsem("cst%d" % i) for i in range(3)]
        load("sp", cF[:], constF[:, :], [TC], cds[0])
        load("sp", gp[:], gpack[:, :], [TC], cds[1])
        load("pool", cB[:], constB[:, :], [TC], cds[2])

        xst_f, _xt = arF.alloc(4096, 4)
        xst_t = xst_f.rearrange("p (b n) -> p b n", b=4)
        XST = list(_xt)
        xst_ds = [S.dsem("xst%d" % i_) for i_ in range(4)]
        for i, (t0, tn) in enumerate(TOK128):
            b = i % 4
            tt = min(t0 // 512, 4)
            load("sp", xst_t[0:tn, b, :], xin[t0:t0 + tn, :], [XST[b]], xst_ds[b])
            for half in range(2):
                bk = nextbank()

                def f(e, tn=tn, b=b, half=half, bk=bk):
                    r = None
                    for j in range(4):
                        c = half * 4 + j
                        r = e.transpose(bank(bk)[:, j * 128:j * 128 + tn], xst_t[0:tn, b, c * 128:(c + 1) * 128], ident[0:tn, 0:tn])
                    return r
                S.op("pe", f, reads=[XST[b], TC], writes=[PB[bk]])
                src = bank(bk).rearrange("p (c t) -> p c t", c=4)[:, :, 0:tn]
                dst = xT[:, half * 4:half * 4 + 4, t0:t0 + tn]
                eng = "dve" if half == 0 else "act"
                if eng == "dve":
                    S.op("dve", lambda e, d=dst, s=src: e.tensor_copy(out=d, in_=s), reads=[PB[bk]], writes=[X[c][tt] for c in range(half * 4, half * 4 + 4)])
                else:
                    S.op("act", lambda e, d=dst, s=src: e.activation(out=d, in_=s, func=AF.Copy), reads=[PB[bk]], writes=[X[c][tt] for c in range(half * 4, half * 4 + 4)])

        def rstd_from_bank(bk, n, scale, k):
            S.op("act", lambda e: e.activation(out=rstd_t[:, k, 0:n], in_=bank(bk)[:, 0:n], func=AF.Ln, bias=EPS, scale=scale), reads=[PB[bk]], writes=[RSTD[k]])
            S.op("act", lambda e: e.activation(out=rstd_t[:, k, 0:n], in_=rstd_t[:, k, 0:n], func=AF.Exp, scale=-0.5), reads=[RSTD[k]], writes=[RSTD[k]])

        norm_i = [0]

        def norm_to_h(gcol, out_fp32=False, after=None):
            ks = {}

            def pa(tt):
                t0, tn = TT[tt]
                bk = nextbank()
                k = norm_i[0] % 2
                norm_i[0] += 1
                ks[tt] = k
                for c in range(8):
                    s_ = c % 2
                    S.op("act", lambda e: e.activation(out=sq_t[:, s_, 0:tn], in_=xT[:, c, t0:t0 + tn], func=AF.Square), reads=[X[c][tt]], writes=[SQ[s_]])
                    S.op("pe", lambda e: e.matmul(bank(bk)[:, 0:tn], lhsT=ones, rhs=sq_t[:, s_, 0:tn], start=(c == 0), stop=(c == 7)), reads=[SQ[s_], TC], writes=[PB[bk]])
                rstd_from_bank(bk, tn, 1.0 / 1024, k)

            def pb(tt):
                t0, tn = TT[tt]
                k = ks[tt]
                for c in range(8):
                    if out_fp32:
                        S.op("dve", lambda e: e.scalar_tensor_tensor(out=xT[:, c, t0:t0 + tn], in0=xT[:, c, t0:t0 + tn], scalar=gp[:, gcol + c:gcol + c + 1], in1=rstd_t[:, k, 0:tn], op0=ALU.mult, op1=ALU.mult),
                             reads=[X[c][tt], RSTD[k], TC], writes=[X[c][tt]])
                    else:
                        S.op("dve", lambda e: e.scalar_tensor_tensor(out=hT[:, c, t0:t0 + tn], in0=xT[:, c, t0:t0 + tn], scalar=gp[:, gcol + c:gcol + c + 1], in1=rstd_t[:, k, 0:tn], op0=ALU.mult, op1=ALU.mult),
                             reads=[X[c][tt], RSTD[k], TC], writes=[H[c][tt]])

            pa(0)
            for tt in range(len(TT)):
                if tt + 1 < len(TT):
                    pa(tt + 1)
                pb(tt)
                if after is not None:
                    after(tt)

        def add_to_x(bk, dc, tt, eng="dve"):
            t0, tn = TT[tt]
            S.op(eng, lambda e: e.tensor_tensor(out=xT[:, dc, t0:t0 + tn], in0=xT[:, dc, t0:t0 + tn], in1=bank(bk)[:, 0:tn], op=ALU.add), reads=[PB[bk], X[dc][tt]], writes=[X[dc][tt]])

        wsl_ds = [S.dsem("wsl0"), S.dsem("wsl1")]
        wo_ds = [S.dsem("wo0"), S.dsem("wo1")]
        vc_ds = S.dsem("vc")
        kc_ds = [S.dsem("kc0"), S.dsem("kc1")]
        eb_ds = S.dsem("eb")
        ebs_ds = S.dsem("ebs")
        ring_ds = [[S.dsem("rg%d_%d" % (s_, j_)) for j_ in range(3)] for s_ in range(2)]
        pst_ds = [S.dsem("pst0"), S.dsem("pst1")]
        wg_ds = S.dsem("wg")
        wp_ds = S.dsem("wp")
        try:
            for l in range(2):
                g0 = 32 * l
                stop_at(1)
                for g in range(2):
                    arB.reset()
                    arF.reset()
                    wsl = []
                    WSL = []
                    for b in range(2):
                        ap, tk = arB.alloc(8 * 3 * 128)
                        wsl.append(ap.rearrange("p (c j n) -> p c j n", c=8, j=3))
                        WSL.append(tk)
                    wo = []
                    WO = []
                    for b in range(2):
                        ap, tk = arB.alloc(1024)
                        wo.append(ap)
                        WO.append(tk)
                    qPb, Qb, kPb, Kb, vPb, Vb = [], [], [], [], [], []
                    for b_ in range(2):
                        ap, tk = arB.alloc(T, 5)
                        qPb.append(ap)
                        Qb.append(tk)
                        ap, tk = arB.alloc(T, 5)
                        kPb.append(ap)
                        Kb.append(tk)
                        ap, tk = arB.alloc(17 * 128, 17)
                        vPb.append(ap.rearrange("p (i n) -> p i n", i=17))
                        Vb.append(tk)
                    mixP, MX = arB.alloc(T, 5)
                    kcT, KCT = arB.alloc(2048, 4)
                    vc_f, VC = arB.alloc(16 * 128)
                    vc = vc_f.rearrange("p (k n) -> p k n", k=16)
                    kc = []
                    KC = []
                    for b in range(2):
                        ap, tk = arF.alloc(512)
                        kc.append(ap.rearrange("p (j n) -> p j n", j=4))
                        KC.append(tk)
                    if g == 0:
                        osb, OSB = arF.alloc(512)
                    oss, OSS = arF.alloc(16)
                    sqh, SQH = arB.alloc(512)
                    if g == 0:
                        e_t = []
                        E = []
                        for b in range(2):
                            ap, tk = arF.alloc(1024)
                            e_t.append(ap)
                            E.append(tk)
                        Lp = []
                        LP = []
                        wbf = []
                        WB = []
                        for b in range(2):
                            ap, tk = arB.alloc(1024)
                            Lp.append(ap)
                            LP.append(tk)
                            ap, tk = arB.alloc(1024)
                            wbf.append(ap)
                            WB.append(tk)
                        LaccB, LACC = arB.alloc(1024)
                        es_f, ES = arF.alloc(2 * 272)
                        es_t = es_f.rearrange("p (h c) -> p h c", h=2)
                        Ls_f, LS = arB.alloc(2 * 272)
                        Ls = Ls_f.rearrange("p (h c) -> p h c", h=2)
                        suf_f, SUF = arB.alloc(2 * 256)
                        suf = suf_f.rearrange("p (h c) -> p h c", h=2)
                        ws_f, WS = arB.alloc(2 * 272)
                        ws = ws_f.rearrange("p (h c) -> p h c", h=2)
                    else:
                        EB_f, EB = arF.alloc(2 * 640)
                        EBt = EB_f.rearrange("p (h c) -> p h c", h=2)
                        bh_f, BH = arB.alloc(2 * 640)
                        biasH = bh_f.rearrange("p (h c) -> p h c", h=2)
                        bl_f, BL = arB.alloc(2 * 640)
                        biasL = bl_f.rearrange("p (h c) -> p h c", h=2)
                        pcb = []
                        PCB = []
                        for b_ in range(2):
                            ap, tk = arB.alloc(2 * 640)
                            pcb.append(ap.rearrange("p (h c) -> p h c", h=2))
                            PCB.append(tk)
                        od_l = []
                        OSB_l = []
                        for b_ in range(2):
                            ap, tk = arF.alloc(1024)
                            od_l.append(ap.rearrange("p (a c) -> p a c", a=2))
                            OSB_l.append(tk)
                        EBS_f, EBSK = arF.alloc(160)
                        EBS = EBS_f.rearrange("p (h c) -> p h c", h=2)
                        ecs_f, ECS = arF.alloc(160)
                        ecs = ecs_f.rearrange("p (h c) -> p h c", h=2)
                        pcs_f, PCS = arB.alloc(160)
                        pcs = pcs_f.rearrange("p (h c) -> p h c", h=2)
                        recs, RECS = arF.alloc(16)

                    def head_norm(src, SRC, t0, n, gcol, tt, bk=None):
                        S.op("dve", lambda e: e.tensor_tensor(out=sqh[:, 0:n], in0=src[:, 0:n], in1=src[:, 0:n], op=ALU.mult), reads=[SRC], writes=[SQH])
                        if bk is None:
                            bk = nextbank(6, 8)
                        S.op("pe", lambda e: e.matmul(bank(bk)[:, 0:n], lhsT=blk64, rhs=sqh[:, 0:n], start=True, stop=True), reads=[SQH, TC], writes=[PB[bk]])
                        k = norm_i[0] % 2
                        norm_i[0] += 1
                        rstd_from_bank(bk, n, 1.0 / 64, k)
                        S.op("dve", lambda e: e.scalar_tensor_tensor(out=mixP[:, t0:t0 + n], in0=src[:, 0:n], scalar=gp[:, gcol:gcol + 1], in1=rstd_t[:, k, 0:n], op0=ALU.mult, op1=ALU.mult),
                             reads=[SRC, RSTD[k], TC], writes=[MX[tt]])

                    def issue_loads(pq):
                        b = pq % 2
                        for j in range(3):
                            c0 = g * 1536 + j * 512 + pq * 128
                            src = w_in[l, :, c0:c0 + 128].rearrange("(c p) n -> p c n", p=128)
                            load("pool", wsl[b][:, :, j, :], src, [WSL[b]], wsl_ds[b])
                        load("pool", wo[b], w_out[l, (g * 4 + pq) * 128:(g * 4 + pq + 1) * 128, :], [WO[b]], wo_ds[b])

                    def qkv_gen(pn, banks, ev_v):
                        bb = pn % 2
                        qPn, kPn, vPn, Qn, Kn, Vn = qPb[bb], kPb[bb], vPb[bb], Qb[bb], Kb[bb], Vb[bb]
                        bi = [0]

                        def nb_():
                            x_ = banks[bi[0] % len(banks)]
                            bi[0] += 1
                            return x_
                        def fm(tt):
                            t0, tn = TT[tt]
                            for j in range(2):
                                bk = nb_()

                                def f(e):
                                    r = None
                                    for c in range(8):
                                        r = e.matmul(bank(bk)[:, 0:tn], lhsT=wsl[bb][:, c, j, :], rhs=hT[:, c, t0:t0 + tn], start=(c == 0), stop=(c == 7))
                                    return r
                                S.op("pe", f, reads=[WSL[bb]] + [H[c][tt] for c in range(8)], writes=[PB[bk]])
                                if j == 0:
                                    S.op("act", lambda e: e.activation(out=qPn[:, t0:t0 + tn], in_=bank(bk)[:, 0:tn], func=AF.Copy, scale=0.125), reads=[PB[bk]], writes=[Qn[tt]])
                                else:
                                    S.op("dve", lambda e: e.tensor_copy(out=kPn[:, t0:t0 + tn], in_=bank(bk)[:, 0:tn]), reads=[PB[bk]], writes=[Kn[tt]])
                                yield
                        def tm(i):
                            t0, tn = TOK128[i]
                            tt = min(t0 // 512, 4)
                            want_out = (g == 0) or (i >= 12)
                            bk = nb_()
                            if want_out:
                                def f(e):
                                    r = None
                                    for c in range(8):
                                        r = e.matmul(bank(bk)[0:tn, 0:256], lhsT=hT[:, c, t0:t0 + tn], rhs=wsl[bb][:, c, 1:3, :], start=(c == 0), stop=(c == 7))
                                    return r
                                voff = 128
                            else:
                                def f(e):
                                    r = None
                                    for c in range(8):
                                        r = e.matmul(bank(bk)[0:tn, 0:128], lhsT=hT[:, c, t0:t0 + tn], rhs=wsl[bb][:, c, 2, :], start=(c == 0), stop=(c == 7))
                                    return r
                                voff = 0
                            S.op("pe", f, reads=[WSL[bb]] + [H[c][tt] for c in range(8)], writes=[PB[bk]])
                            if ev_v == "act":
                                S.op("act", lambda e: e.activation(out=vPn[0:tn, i, :], in_=bank(bk)[0:tn, voff:voff + 128], func=AF.Copy), reads=[PB[bk]], writes=[Vn[i]])
                            else:
                                S.op("dve", lambda e: e.tensor_copy(out=vPn[0:tn, i, :], in_=bank(bk)[0:tn, voff:voff + 128]), reads=[PB[bk]], writes=[Vn[i]])
                            if want_out:
                                si = stg_i[0] % 3
                                stg_i[0] += 1
                                S.op("dve", lambda e: e.tensor_copy(out=stg_t[0:tn, si, 0:256], in_=bank(bk)[0:tn, 0:256]), reads=[PB[bk]], writes=[STG[si]])
                                if g == 0:
                                    ko = sbk[l, t0:t0 + tn, pn * 128:(pn + 1) * 128]
                                    vo = sbv[l, t0:t0 + tn, pn * 128:(pn + 1) * 128]
                                else:
                                    r0 = t0 - 1536
                                    ko = cbk[l, r0:r0 + tn, pn * 128:(pn + 1) * 128]
                                    vo = cbv[l, r0:r0 + tn, pn * 128:(pn + 1) * 128]
                                store(ko, stg_t[0:tn, si, 0:128], [STG[si]], stg_ds[si])
                                store(vo, stg_t[0:tn, si, 128:256], [STG[si]], stg_ds[si])
                            yield

                        for tt in range(5):
                            yield from fm(tt)
                            for i in (range(4 * tt, 4 * tt + 4) if tt < 4 else [16]):
                                yield from tm(i)

                    issue_loads(0)
                    issue_loads(1)
                    if g == 0:
                        S.bg2 = qkv_gen(0, list(range(8)), "act")
                        norm_to_h(g0 + 0, after=lambda tt: [S.tick() for _ in range(2 + (4 if tt < 4 else 1))])
                        S.drain_bg2()
                    else:
                        for _ in qkv_gen(0, list(range(8)), "act"):
                            pass
                    for pq in range(4):
                        b = pq % 2
                        qP, kP, vP, Q, K, V = qPb[b], kPb[b], vPb[b], Qb[b], Kb[b], Vb[b]
                        gcol_o = g0 + 24 + g * 4 + pq
                        if pq + 1 < 4:
                            S.bg2 = qkv_gen(pq + 1, [3] if g == 0 else [0, 1, 2, 3, 6, 7], "dve" if g == 0 else "act")
                        stop_at(2)
                        wo_i = [0]

                        def wout_gen(tts):
                            for tt in tts:
                                t0, tn = TT[tt]
                                for dc in range(8):
                                    bk = 4 + (wo_i[0] % 2)
                                    wo_i[0] += 1
                                    S.op("pe", lambda e: e.matmul(bank(bk)[:, 0:tn], lhsT=wo[b][:, dc * 128:(dc + 1) * 128], rhs=mixP[:, t0:t0 + tn], start=True, stop=True), reads=[WO[b], MX[tt]], writes=[PB[bk]])
                                    add_to_x(bk, dc, tt)
                                    yield

                        if g == 0:
                            steps = []
                            for qi in range(4):
                                for kb in range(4 * qi + 3, -1, -1):
                                    steps.append((qi, kb))

                            Mtri = Mr[0][:, 0:128]
                            zv = [pp[0].rearrange("p (h c) -> p h c", h=2), pp[0].rearrange("p (h c) -> p h c", h=2)]
                            osets = [(pp[3], 6), (pp[3], 6)]
                            av = pp[2].rearrange("p (h c) -> p h c", h=2)
                            ev = [x_.rearrange("p (h c) -> p h c", h=2) for x_ in e_t]
                            lv = [x_.rearrange("p (h c) -> p h c", h=2) for x_ in Lp]
                            wv = [x_.rearrange("p (h c) -> p h c", h=2) for x_ in wbf]
                            lacv = LaccB.rearrange("p (h c) -> p h c", h=2)

                            def geom(i):
                                qi, kb = steps[i]
                                r = kb - 4 * qi
                                c0 = 128 * r if r >= 0 else 0
                                return qi, kb, r, c0, i % 2, qi * 512

                            def s1a(i):
                                qi, kb, r, c0, zi, t0 = geom(i)

                                def f(e):
                                    dg = (r >= 0)
                                    e.matmul(zv[zi][:, 0, c0:512], lhsT=kP[0:64, kb * 128:(kb + 1) * 128], rhs=qP[0:64, t0 + c0:t0 + 512], start=True, stop=not dg, skip_group_check=True)
                                    rr = e.matmul(zv[zi][:, 1, c0:512], lhsT=kP[64:128, kb * 128:(kb + 1) * 128], rhs=qP[64:128, t0 + c0:t0 + 512], start=True, stop=not dg, skip_group_check=True)
                                    if dg:
                                        for h in range(2):
                                            rr = e.matmul(zv[zi][:, h, c0:c0 + 128], lhsT=identB, rhs=NEGM, start=False, stop=True, skip_group_check=True)
                                    return rr
                                S.op("pe", f, reads=[K[kb // 4], Q[qi], TC], writes=[PB[0], PB[1]])
                                S.op("act", lambda e: e.activation(out=ev[zi][:, :, c0:512], in_=zv[zi][:, :, c0:512], func=AF.Exp), reads=[PB[0], PB[1]], writes=[E[zi]])

                            def s1b(i):
                                qi, kb, r, c0, zi, t0 = geom(i)
                                S.op("act", lambda e: e.activation(out=lv[zi][:, :, c0:512], in_=ev[zi][:, :, c0:512], func=AF.Ln, bias=1.0), reads=[E[zi]], writes=[LP[zi]])

                            def s2a(i):
                                qi, kb, r, c0, zi, t0 = geom(i)
                                first = (r == 3)
                                c1 = c0 + 128 if r >= 0 else 0

                                def f(e):
                                    rr = None
                                    for h in range(2):
                                        e.matmul(av[:, h, c0:512], lhsT=kP[64 * h:64 * h + 64, kb * 128:(kb + 1) * 128], rhs=qP[64 * h:64 * h + 64, t0 + c0:t0 + 512], start=True, stop=False, skip_group_check=True)
                                    if r >= 0:
                                        for h in range(2):
                                            e.matmul(av[:, h, c0:c0 + 128], lhsT=identB, rhs=NEGM, start=False, stop=False, skip_group_check=True)
                                    for h in range(2):
                                        rr = e.matmul(av[:, h, c0:512], lhsT=negTri, rhs=lv[zi][:, h, c0:512], start=False, stop=first, skip_group_check=True)
                                    return rr
                                S.op("pe", f, reads=[K[kb // 4], Q[qi], LP[zi], TC], writes=[PB[4], PB[5]])
                                if not first:
                                    def f2(e):
                                        rr = None
                                        for h in range(2):
                                            rr = e.matmul(av[:, h, c1:512], lhsT=negOnes, rhs=lacv[:, h, c1:512], start=False, stop=True, skip_group_check=True)
                                        return rr
                                    S.op("pe", f2, reads=[LACC, TC], writes=[PB[4], PB[5]])
                                S.op("act", lambda e: e.activation(out=wv[zi][:, :, c0:512], in_=av[:, :, c0:512], func=AF.Exp), reads=[PB[4], PB[5]], writes=[WB[zi]])

                            def s2b(i):
                                qi, kb, r, c0, zi, t0 = geom(i)
                                first = (r == 3)
                                c1 = c0 + 128 if r >= 0 else 0
                                if kb > 0:
                                    if r >= 0:
                                        S.op("dve", lambda e: e.tensor_copy(out=lacv[:, :, c0:c0 + 128], in_=lv[zi][:, :, c0:c0 + 128]), reads=[LP[zi]], writes=[LACC])
                                    if c1 < 512:
                                        S.op("dve", lambda e: e.tensor_tensor(out=lacv[:, :, c1:512], in0=lacv[:, :, c1:512], in1=lv[zi][:, :, c1:512], op=ALU.add), reads=[LP[zi], LACC], writes=[LACC])

                            def s2c(i):
                                qi, kb, r, c0, zi, t0 = geom(i)
                                first = (r == 3)

                                ot, ob0 = osets[qi % 2]

                                def f(e):
                                    e.matmul(ot[:, c0:512], lhsT=vP[:, kb, :], rhs=wv[zi][:, 0, c0:512], start=first, stop=(kb == 0), skip_group_check=True)
                                    return e.matmul(ot[:, 512 + c0:1024], lhsT=vP[:, kb, :], rhs=wv[zi][:, 1, c0:512], start=first, stop=(kb == 0), skip_group_check=True)
                                S.op("pe", f, reads=[V[kb], WB[zi]], writes=[PB[ob0], PB[ob0 + 1]])
                                if kb == 0:
                                    S.op("dve", lambda e: e.tensor_copy(out=osb[0:64, 0:512], in_=ot[0:64, 0:512]), reads=[PB[ob0]], writes=[OSB])
                                    S.op("dve", lambda e: e.tensor_copy(out=osb[64:128, 0:512], in_=ot[64:128, 512:1024]), reads=[PB[ob0 + 1]], writes=[OSB])
                                    pending.append([3, (t0, qi, ob0)])

                            NFILL = int(os.environ.get("KFILL", "3"))

                            def fill(i):
                                zi2 = i % 2

                                def f(e):
                                    rr = None
                                    for j in range(NFILL):
                                        rr = e.matmul(zv[zi2][:, j % 2, 0:512], lhsT=negTri, rhs=Mr[0], start=True, stop=True)
                                    return rr
                                S.op("pe", f, reads=[TC], writes=[PB[0], PB[1]])

                            n = len(steps)
                            pending = []
                            s1a(0)
                            s1b(0)
                            for i in range(n):
                                if i + 1 < n:
                                    s1a(i + 1)
                                s2a(i)
                                s2b(i)
                                if i >= 1:
                                    s2c(i - 1)
                                if i + 1 < n:
                                    s1b(i + 1)
                                if i >= 1 and S.tick():
                                    pass
                                elif NFILL and 1 <= i < n - 2:
                                    fill(i)
                                for p_ in list(pending):
                                    p_[0] -= 1
                                    if p_[0] <= 0:
                                        pending.remove(p_)
                                        head_norm(osb, OSB, p_[1][0], 512, gcol_o, p_[1][1], bk=2)
                            s2c(n - 1)
                            for p_ in pending:
                                head_norm(osb, OSB, p_[1][0], 512, gcol_o, p_[1][1], bk=2)
                            S.drain_bg2()

                            stop_at(3)
                            S.bg = wout_gen([0, 1, 2, 3])
                            load("pool", vc, csv[l, :, pq * 128:(pq + 1) * 128].rearrange("(k s) n -> s k n", s=128), [VC], vc_ds)
                            for cq in range(4):
                                kb_ = cq % 2
                                load("sp", kc[kb_], csk[l, cq * 512:(cq + 1) * 512, pq * 128:(pq + 1) * 128].rearrange("(j s) n -> s j n", s=128), [KC[kb_]], kc_ds[kb_])
                                bk = nextbank(6, 8)

                                def f(e, kb_=kb_, bk=bk):
                                    r = None
                                    for j in range(4):
                                        r = e.transpose(bank(bk)[:, j * 128:(j + 1) * 128], kc[kb_][:, j, :], ident)
                                    return r
                                S.op("pe", f, reads=[KC[kb_], TC], writes=[PB[bk]])
                                S.op("act", lambda e, cq=cq, bk=bk: e.activation(out=kcT[:, cq * 512:(cq + 1) * 512], in_=bank(bk)[:, 0:512], func=AF.Copy), reads=[PB[bk]], writes=[KCT[cq]])

                            zsv = pp[0].rearrange("p (h c) -> p h c", h=2)
                            asv = pp[1].rearrange("p (h c) -> p h c", h=2)

                            def zmm(e, dst, stop):
                                r = None
                                for h in range(2):
                                    for kb in range(16):
                                        st = True if stop else (kb == 0)
                                        e.matmul(dst[:, h, kb * 16:(kb + 1) * 16], lhsT=kcT[64 * h:64 * h + 64, kb * 128:(kb + 1) * 128], rhs=qP[64 * h:64 * h + 64, 2048:2064], start=st, stop=stop, skip_group_check=True)
                                    r = e.matmul(dst[0:16, h, 256:272], lhsT=kP[64 * h:64 * h + 64, 2048:2064], rhs=qP[64 * h:64 * h + 64, 2048:2064], start=bool(stop), stop=stop, skip_group_check=True)
                                return r
                            S.op("pe", lambda e: zmm(e, zsv, True), reads=KCT + [K[4], Q[4]], writes=[PB[0], PB[1]])
                            S.op("act", lambda e: e.activation(out=es_t[:, :, 0:256], in_=zsv[:, :, 0:256], func=AF.Exp), reads=[PB[0], PB[1]], writes=[ES])
                            S.op("act", lambda e: e.activation(out=es_t[0:16, :, 256:272], in_=zsv[0:16, :, 256:272], func=AF.Exp), reads=[PB[0], PB[1]], writes=[ES])
                            S.op("act", lambda e: e.activation(out=Ls[:, :, 0:256], in_=es_t[:, :, 0:256], func=AF.Ln, bias=1.0), reads=[ES], writes=[LS])
                            S.op("act", lambda e: e.activation(out=Ls[0:16, :, 256:272], in_=es_t[0:16, :, 256:272], func=AF.Ln, bias=1.0), reads=[ES], writes=[LS])
                            for h in range(2):
                                S.op("pool", lambda e, h=h: e.tensor_tensor(out=Ls[0:16, h, 256:272], in0=Ls[0:16, h, 256:272], in1=M16, op=ALU.mult), reads=[LS, TC], writes=[LS])
                            S.op("pool", lambda e: e.memset(suf[:, :, 240:256], 0.0), writes=[SUF])
                            for kb in range(14, -1, -1):
                                S.op("pool", lambda e, kb=kb: e.tensor_tensor(out=suf[:, :, kb * 16:(kb + 1) * 16], in0=suf[:, :, (kb + 1) * 16:(kb + 2) * 16], in1=Ls[:, :, (kb + 1) * 16:(kb + 2) * 16], op=ALU.add), reads=[SUF, LS], writes=[SUF])

                            def amm(e):
                                zmm(e, asv, False)
                                r = None
                                for h in range(2):
                                    e.matmul(asv[:, h, 0:256], lhsT=negTri, rhs=Ls[:, h, 0:256], start=False, stop=False, skip_group_check=True)
                                    e.matmul(asv[:, h, 0:256], lhsT=negOnes, rhs=suf[:, h, 0:256], start=False, stop=False, skip_group_check=True)
                                    for kb in range(16):
                                        e.matmul(asv[:, h, kb * 16:(kb + 1) * 16], lhsT=negOnes[0:16, :], rhs=Ls[0:16, h, 256:272], start=False, stop=True, skip_group_check=True)
                                    r = e.matmul(asv[0:16, h, 256:272], lhsT=negTri[0:16, 0:16], rhs=Ls[0:16, h, 256:272], start=False, stop=True, skip_group_check=True)
                                return r
                            S.op("pe", amm, reads=KCT + [K[4], Q[4], LS, SUF, TC], writes=[PB[2], PB[3]])
                            S.op("act", lambda e: e.activation(out=ws[:, :, 0:256], in_=asv[:, :, 0:256], func=AF.Exp), reads=[PB[2], PB[3]], writes=[WS])
                            S.op("act", lambda e: e.activation(out=ws[0:16, :, 256:272], in_=asv[0:16, :, 256:272], func=AF.Exp), reads=[PB[2], PB[3]], writes=[WS])
                            for h in range(2):
                                S.op("pool", lambda e, h=h: e.tensor_tensor(out=ws[0:16, h, 256:272], in0=ws[0:16, h, 256:272], in1=M16, op=ALU.mult), reads=[WS, TC], writes=[WS])

                            def avs(e):
                                r = None
                                for h in range(2):
                                    o = pp[3][:, h * 512:h * 512 + 16]
                                    for kb in range(16):
                                        e.matmul(o, lhsT=vc[:, kb, :], rhs=ws[:, h, kb * 16:(kb + 1) * 16], start=(kb == 0), stop=False)
                                    r = e.matmul(o, lhsT=vP[0:16, 16, :], rhs=ws[0:16, h, 256:272], start=False, stop=True)
                                return r
                            S.op("pe", avs, reads=[VC, V[16], WS], writes=[PB[6], PB[7]])
                            S.op("act", lambda e: e.activation(out=oss[0:64, 0:16], in_=pp[3][0:64, 0:16], func=AF.Copy), reads=[PB[6]], writes=[OSS])
                            S.op("act", lambda e: e.activation(out=oss[64:128, 0:16], in_=pp[3][64:128, 512:528], func=AF.Copy), reads=[PB[7]], writes=[OSS])
                            head_norm(oss, OSS, 2048, 16, gcol_o, 4)
                        else:
                            load("sp", EBt, biasP[l, 2 * pq:2 * pq + 2].rearrange("h p c -> p h c"), [EB], eb_ds)
                            for h in range(2):
                                S.op("dve", lambda e, h=h: e.tensor_tensor(out=EBt[:, h, :], in0=EBt[:, h, :], in1=maskc, op=ALU.add), reads=[EB, TC], writes=[EB])
                            S.op("dve", lambda e: e.tensor_copy(out=biasH[:, :, :], in_=EBt[:, :, :]), reads=[EB], writes=[BH])
                            S.op("dve", lambda e: e.tensor_tensor(out=biasL[:, :, :], in0=EBt[:, :, :], in1=biasH[:, :, :], op=ALU.subtract), reads=[EB, BH], writes=[BL])
                            load("sp", EBS, biasS[l, pq].rearrange("p (h c) -> p h c", h=2), [EBSK], ebs_ds)
                            S.op("act", lambda e: e.activation(out=EBS[:, :, :], in_=EBS[:, :, :], func=AF.Exp), reads=[EBSK], writes=[EBSK])
                            xyv = [pp[0].rearrange("p (h c) -> p h c", h=2), pp[1].rearrange("p (h c) -> p h c", h=2)]
                            zv2 = pp[2].rearrange("p (h c) -> p h c", h=2)

                            def cb_s(m):
                                nb = min(m, 4) + 1
                                k_ = m % 2
                                w4 = min(nb, 4) * 128

                                def f(e):
                                    r = None
                                    for h in range(2):
                                        for o in range(nb):
                                            if o < 4:
                                                dst = xyv[k_][:, h, o * 128:(o + 1) * 128]
                                                st = (o == 0)
                                            else:
                                                dst = zv2[:, h, k_ * 128:(k_ + 1) * 128]
                                                st = True
                                            e.matmul(dst, lhsT=kP[64 * h:64 * h + 64, (m - o) * 128:(m - o + 1) * 128], rhs=qP[64 * h:64 * h + 64, m * 128:(m + 1) * 128], start=st, stop=False, skip_group_check=True)
                                    for h in range(2):
                                        e.matmul(xyv[k_][:, h, 0:w4], lhsT=identB, rhs=biasH[:, h, 0:w4], start=False, stop=False, skip_group_check=True)
                                        r = e.matmul(xyv[k_][:, h, 0:w4], lhsT=identB, rhs=biasL[:, h, 0:w4], start=False, stop=True, skip_group_check=True)
                                        if nb == 5:
                                            e.matmul(zv2[:, h, k_ * 128:(k_ + 1) * 128], lhsT=identB, rhs=biasH[:, h, 512:640], start=False, stop=False, skip_group_check=True)
                                            r = e.matmul(zv2[:, h, k_ * 128:(k_ + 1) * 128], lhsT=identB, rhs=biasL[:, h, 512:640], start=False, stop=True, skip_group_check=True)
                                    return r
                                wr = [PB[2 * k_], PB[2 * k_ + 1]] + ([PB[4], PB[5]] if nb == 5 else [])
                                S.op("pe", f, reads=[K[m // 4], K[max(m - 4, 0) // 4], Q[m // 4], BH, BL, TC], writes=wr)
                                if nb == 5:
                                    S.op("act", lambda e: e.activation(out=pcb[k_][:, :, 512:640], in_=zv2[:, :, k_ * 128:(k_ + 1) * 128], func=AF.Exp), reads=[PB[4], PB[5]], writes=[PCB[k_]])
                                S.op("act", lambda e: e.activation(out=pcb[k_][:, :, 0:w4], in_=xyv[k_][:, :, 0:w4], func=AF.Exp), reads=[PB[2 * k_], PB[2 * k_ + 1]], writes=[PCB[k_]])

                            def cb_mul(m):
                                pass

                            def cb_av(m):
                                nb = min(m, 4) + 1
                                k_ = m % 2
                                ob = 6 + (m % 2)

                                def f2(e):
                                    r = None
                                    if os.environ.get("KAV2D"):
                                        for h in range(2):
                                            for o in range(nb):
                                                e.matmul(bank(ob)[:, h * 128:(h + 1) * 128], lhsT=vP[:, m - o, :], rhs=pcb[k_][:, h, o * 128:(o + 1) * 128], start=(o == 0), stop=(o == nb - 1))
                                        for h in range(2):
                                            for o in range(nb):
                                                r = e.matmul(bank(ob)[:, 256 + h * 128:256 + (h + 1) * 128], lhsT=ones, rhs=pcb[k_][:, h, o * 128:(o + 1) * 128], start=(o == 0), stop=(o == nb - 1))
                                        return r
                                    for o in range(nb):
                                        e.matmul(bank(ob)[:, 0:256], lhsT=vP[:, m - o, :], rhs=pcb[k_][:, :, o * 128:(o + 1) * 128], start=(o == 0), stop=(o == nb - 1))
                                    for o in range(nb):
                                        r = e.matmul(bank(ob)[:, 256:512], lhsT=ones, rhs=pcb[k_][:, :, o * 128:(o + 1) * 128], start=(o == 0), stop=(o == nb - 1))
                                    return r
                                S.op("pe", f2, reads=[PCB[k_], TC] + [V[m - o] for o in range(nb)], writes=[PB[ob]])
                                mc = (m % 4) * 128
                                od = od_l[(m // 4) % 2]
                                OSB = OSB_l[(m // 4) % 2]
                                v4 = bank(ob).rearrange("p (a b c) -> p a b c", a=2, b=2)
                                for h in range(2):
                                    S.op("dve", lambda e, h=h: e.tensor_copy(out=od[64 * h:64 * h + 64, :, mc:mc + 128], in_=v4[64 * h:64 * h + 64, :, h, :]), reads=[PB[ob]], writes=[OSB])

                            def cb_norm(g4):
                                od = od_l[g4 % 2]
                                OSB = OSB_l[g4 % 2]
                                S.op("act", lambda e: e.activation(out=od[:, 1, :], in_=od[:, 1, :], func=AF.Ln), reads=[OSB], writes=[OSB])
                                S.op("act", lambda e: e.activation(out=od[:, 1, :], in_=od[:, 1, :], func=AF.Exp, scale=-1.0), reads=[OSB], writes=[OSB])
                                S.op("dve", lambda e: e.tensor_tensor(out=od[:, 0, :], in0=od[:, 0, :], in1=od[:, 1, :], op=ALU.mult), reads=[OSB], writes=[OSB])
                                head_norm(od[:, 0, :], OSB, g4 * 512, 512, gcol_o, g4, bk=7)

                            NM = int(os.environ.get("KCBM", "16"))
                            cb_s(0)
                            cb_mul(0)
                            for m in range(NM):
                                if m + 1 < NM:
                                    cb_s(m + 1)
                                    cb_mul(m + 1)
                                cb_av(m)
                                if m % 4 == 0 and m > 0:
                                    cb_norm(m // 4 - 1)
                            cb_norm(NM // 4 - 1)
                            stop_at(6)
                            if pq == 3:
                                S.bg = wout_gen([0, 1, 2, 3])
                            load("pool", vc[:, 0:4, :], ccv[l, :, pq * 128:(pq + 1) * 128].rearrange("(k s) n -> s k n", s=128), [VC], vc_ds)
                            load("sp", kc[0], cck[l, :, pq * 128:(pq + 1) * 128].rearrange("(j s) n -> s j n", s=128), [KC[0]], kc_ds[0])
                            bk = nextbank(6, 8)

                            def f(e, bk=bk):
                                r = None
                                for j in range(4):
                                    r = e.transpose(bank(bk)[:, j * 128:(j + 1) * 128], kc[0][:, j, :], ident)
                                return r
                            S.op("pe", f, reads=[KC[0], TC], writes=[PB[bk]])
                            S.op("dve", lambda e, bk=bk: e.tensor_copy(out=kcT[:, 0:512], in_=bank(bk)[:, 0:512]), reads=[PB[bk]], writes=[KCT[0]])
                            zsv = pp[0].rearrange("p (h c) -> p h c", h=2)

                            def f(e):
                                r = None
                                for h in range(2):
                                    for kb in range(4):
                                        e.matmul(zsv[:, h, kb * 16:(kb + 1) * 16], lhsT=kcT[64 * h:64 * h + 64, kb * 128:(kb + 1) * 128], rhs=qP[64 * h:64 * h + 64, 2048:2064], start=True, stop=True)
                                    r = e.matmul(zsv[0:16, h, 64:80], lhsT=kP[64 * h:64 * h + 64, 2048:2064], rhs=qP[64 * h:64 * h + 64, 2048:2064], start=True, stop=True)
                                return r
                            S.op("pe", f, reads=[KCT[0], K[4], Q[4]], writes=[PB[0], PB[1]])
                            S.op("act", lambda e: e.activation(out=ecs[:, :, 0:64], in_=zsv[:, :, 0:64], func=AF.Exp), reads=[PB[0], PB[1]], writes=[ECS])
                            S.op("act", lambda e: e.activation(out=ecs[0:16, :, 64:80], in_=zsv[0:16, :, 64:80], func=AF.Exp), reads=[PB[0], PB[1]], writes=[ECS])
                            S.op("dve", lambda e: e.tensor_tensor(out=pcs[:, :, 0:64], in0=ecs[:, :, 0:64], in1=EBS[:, :, 0:64], op=ALU.mult), reads=[ECS, EBSK], writes=[PCS])
                            S.op("dve", lambda e: e.tensor_tensor(out=pcs[0:16, :, 64:80], in0=ecs[0:16, :, 64:80], in1=EBS[0:16, :, 64:80], op=ALU.mult), reads=[ECS, EBSK], writes=[PCS])

                            def f(e):
                                r = None
                                for h in range(2):
                                    o = pp[3][:, h * 512:h * 512 + 16]
                                    d = pp[3][:, h * 512 + 16:h * 512 + 32]
                                    for kb in range(4):
                                        e.matmul(o, lhsT=vc[:, kb, :], rhs=pcs[:, h, kb * 16:(kb + 1) * 16], start=(kb == 0), stop=False)
                                    e.matmul(o, lhsT=vP[0:16, 16, :], rhs=pcs[0:16, h, 64:80], start=False, stop=True)
                                    for kb in range(4):
                                        e.matmul(d, lhsT=ones, rhs=pcs[:, h, kb * 16:(kb + 1) * 16], start=(kb == 0), stop=False)
                                    r = e.matmul(d, lhsT=ones[0:16, :], rhs=pcs[0:16, h, 64:80], start=False, stop=True)
                                return r
                            S.op("pe", f, reads=[VC, V[16], PCS, TC], writes=[PB[6], PB[7]])
                            for h in range(2):
                                S.op("dve", lambda e, h=h: e.reciprocal(out=recs[64 * h:64 * h + 64, :], in_=pp[3][64 * h:64 * h + 64, h * 512 + 16:h * 512 + 32]), reads=[PB[6], PB[7]], writes=[RECS])
                            for h in range(2):
                                S.op("dve", lambda e, h=h: e.tensor_tensor(out=oss[64 * h:64 * h + 64, 0:16], in0=pp[3][64 * h:64 * h + 64, h * 512:h * 512 + 16], in1=recs[64 * h:64 * h + 64, :], op=ALU.mult), reads=[PB[6], PB[7], RECS], writes=[OSS])
                            head_norm(oss, OSS, 2048, 16, gcol_o, 4)

                        stop_at(4)
                        if g == 1:
                            stop_at(7)
                        if g == 1 and pq < 3:
                            wg_ = wout_gen([0, 1, 2, 3])
                            while S.tick():
                                next(wg_, None)
                            for _ in wg_:
                                pass
                        S.drain_bg()
                        for _ in wout_gen([4]):
                            pass
                        if pq + 2 < 4:
                            issue_loads(pq + 2)

                stop_at(8)
                arB.reset()
                arF.reset()
                norm_to_h(g0 + 8)
                NS = 2
                ring = []
                RG = []
                for s_ in range(NS):
                    ap, tk = arB.alloc(12288)
                    ring.append(ap)
                    RG.append(tk)
                actT = []
                ACTT = []
                for b in range(2):
                    ap, tk = arB.alloc(4 * 512, 4)
                    actT.append(ap.rearrange("p (f t) -> p f t", f=4))
                    ACTT.append(tk)
                tmp = []
                TMP = []
                for b in range(2):
                    ap, tk = arF.alloc(512)
                    tmp.append(ap)
                    TMP.append(tk)
                pT_f, PT = arB.alloc(2 * T, 5)
                pT = pT_f.rearrange("p (c t) -> p c t", c=2)
                pst = []
                PST = []
                for b in range(2):
                    ap, tk = arF.alloc(256)
                    pst.append(ap)
                    PST.append(tk)
                sig = []
                SIG = []
                for b in range(2):
                    ap, tk = arF.alloc(512)
                    sig.append(ap)
                    SIG.append(tk)

                def p_gen():
                    for i, (t0, tn) in enumerate(TOK128):
                        b = i % 2
                        tt = min(t0 // 512, 4)
                        load("sp", pst[b][0:tn, :], pin[l, t0:t0 + tn, :], [PST[b]], pst_ds[b])
                        bk = 7

                        def f(e):
                            r = None
                            for c in range(2):
                                r = e.transpose(bank(bk)[:, c * 128:c * 128 + tn], pst[b][0:tn, c * 128:(c + 1) * 128], ident[0:tn, 0:tn])
                            return r
                        S.op("pe", f, reads=[PST[b], TC], writes=[PB[bk]])
                        S.op("dve", lambda e: e.tensor_copy(out=pT[:, :, t0:t0 + tn], in_=bank(bk)[:, 0:256].rearrange("p (c t) -> p c t", c=2)[:, :, 0:tn]), reads=[PB[bk]], writes=[PT[tt]])
                        yield

                groups = [(0, 4), (4, 4), (8, 4), (12, 4), (16, 4), (20, 2)]

                def ffn_load(gi):
                    f0, G = groups[gi]
                    s_ = gi % NS
                    gv = ring[s_][:, 0:4096].rearrange("p (c n) -> p c n", c=8)
                    uv = ring[s_][:, 4096:8192].rearrange("p (c n) -> p c n", c=8)
                    dv = ring[s_][:, 8192:12288].rearrange("p (f n) -> p f n", f=4)
                    load("pool", gv[:, :, 0:G * 128], w_gate[l, :, f0 * 128:(f0 + G) * 128].rearrange("(c p) n -> p c n", p=128), [RG[s_]], ring_ds[s_][0])
                    load("pool", uv[:, :, 0:G * 128], w_up[l, :, f0 * 128:(f0 + G) * 128].rearrange("(c p) n -> p c n", p=128), [RG[s_]], ring_ds[s_][1])
                    load("pool", dv[:, 0:G, :], w_down[l, f0 * 128:(f0 + G) * 128, :].rearrange("(f p) n -> p f n", p=128), [RG[s_]], ring_ds[s_][2])

                ffn_load(0)
                ti = [0]

                def views(gi):
                    s_ = gi % NS
                    return (s_, ring[s_][:, 0:4096].rearrange("p (c n) -> p c n", c=8),
                            ring[s_][:, 4096:8192].rearrange("p (c n) -> p c n", c=8),
                            ring[s_][:, 8192:12288].rearrange("p (f n) -> p f n", f=4))

                def after_group_start(gi):
                    nonlocal_box = None
                    if gi + 1 < len(groups):
                        ffn_load(gi + 1)
                    else:
                        load("pool", wg, w_pg[l].rearrange("(c p) n -> p c n", p=128), [WG], wg_ds)
                        load("pool", wp, w_pp[l].rearrange("(c p) n -> p c n", p=128), [WP], wp_ds)
                        S.bg2 = p_gen()

                wg = ring[0][:, 0:8192].rearrange("p (c n) -> p c n", c=8)
                wp = ring[0][:, 8192:10240].rearrange("p (c n) -> p c n", c=2)
                WG = RG[0]
                WP = RG[0]

                def gu(gi, tt, ab):
                    f0, G = groups[gi]
                    s_, gv, uv, dv = views(gi)
                    t0, tn = TT[tt]
                    for fi in range(G):
                        bg = nextbank(0, 4)
                        bu = nextbank(0, 4)

                        def fg(e):
                            r = None
                            for c in range(8):
                                r = e.matmul(bank(bg)[:, 0:tn], lhsT=gv[:, c, fi * 128:(fi + 1) * 128], rhs=hT[:, c, t0:t0 + tn], start=(c == 0), stop=(c == 7))
                            return r
                        S.op("pe", fg, reads=[RG[s_]] + [H[c][tt] for c in range(8)], writes=[PB[bg]])

                        def fu(e):
                            r = None
                            for c in range(8):
                                r = e.matmul(bank(bu)[:, 0:tn], lhsT=uv[:, c, fi * 128:(fi + 1) * 128], rhs=hT[:, c, t0:t0 + tn], start=(c == 0), stop=(c == 7))
                            return r
                        S.op("pe", fu, reads=[RG[s_]] + [H[c][tt] for c in range(8)], writes=[PB[bu]])
                        tb = ti[0] % 2
                        ti[0] += 1
                        S.op("act", lambda e: e.activation(out=tmp[tb][:, 0:tn], in_=bank(bg)[:, 0:tn], func=AF.Silu), reads=[PB[bg]], writes=[TMP[tb]])
                        S.op("dve", lambda e: e.tensor_tensor(out=actT[ab][:, fi, 0:tn], in0=tmp[tb][:, 0:tn], in1=bank(bu)[:, 0:tn], op=ALU.mult), reads=[PB[bu], TMP[tb]], writes=[ACTT[ab][fi]])

                def dn(gi, tt, ab):
                    f0, G = groups[gi]
                    s_, gv, uv, dv = views(gi)
                    t0, tn = TT[tt]
                    for dc in range(8):
                        bd = nextbank(4, 8)

                        def fd(e):
                            r = None
                            for fi in range(G):
                                r = e.matmul(bank(bd)[:, 0:tn], lhsT=dv[:, fi, dc * 128:(dc + 1) * 128], rhs=actT[ab][:, fi, 0:tn], start=(fi == 0), stop=(fi == G - 1))
                            return r
                        S.op("pe", fd, reads=[RG[s_]] + [ACTT[ab][fi] for fi in range(G)], writes=[PB[bd]])
                        add_to_x(bd, dc, tt)
                        if dc % 2 == 1:
                            S.tick()

                seq = [(gi, tt) for gi in range(len(groups)) for tt in range(len(TT))]
                gu(seq[0][0], seq[0][1], 0)
                after_group_start(0)
                for k in range(len(seq)):
                    gi, tt = seq[k]
                    if k + 1 < len(seq):
                        gu(seq[k + 1][0], seq[k + 1][1], (k + 1) % 2)
                    dn(gi, tt, k % 2)
                    if k + 1 < len(seq) and seq[k + 1][1] == 0:
                        after_group_start(seq[k + 1][0])

                stop_at(9)
                S.drain_bg2()
                norm_to_h(g0 + 16)
                si = 0
                for tt, (t0, tn) in enumerate(TT):
                    for dc in range(8):
                        bgk = nextbank()
                        bpk = nextbank()

                        def fg(e, bk=bgk, dc=dc, t0=t0, tn=tn):
                            r = None
                            for c in range(8):
                                r = e.matmul(bank(bk)[:, 0:tn], lhsT=wg[:, c, dc * 128:(dc + 1) * 128], rhs=hT[:, c, t0:t0 + tn], start=(c == 0), stop=(c == 7))
                            return r
                        S.op("pe", fg, reads=[WG] + [H[c][tt] for c in range(8)], writes=[PB[bgk]])

                        def fp(e, bk=bpk, dc=dc, t0=t0, tn=tn):
                            r = None
                            for c in range(2):
                                r = e.matmul(bank(bk)[:, 0:tn], lhsT=wp[:, c, dc * 128:(dc + 1) * 128], rhs=pT[:, c, t0:t0 + tn], start=(c == 0), stop=(c == 1))
                            return r
                        S.op("pe", fp, reads=[WP, PT[tt]], writes=[PB[bpk]])
                        sb_ = si % 2
                        si += 1
                        S.op("act", lambda e, sb_=sb_, bk=bgk, tn=tn: e.activation(out=sig[sb_][:, 0:tn], in_=bank(bk)[:, 0:tn], func=AF.Sigmoid), reads=[PB[bgk]], writes=[SIG[sb_]])
                        S.op("dve", lambda e, sb_=sb_, bk=bpk, tn=tn: e.tensor_tensor(out=sig[sb_][:, 0:tn], in0=sig[sb_][:, 0:tn], in1=bank(bk)[:, 0:tn], op=ALU.mult), reads=[PB[bpk], SIG[sb_]], writes=[SIG[sb_]])
                        S.op("pool", lambda e, sb_=sb_, dc=dc, t0=t0, tn=tn: e.tensor_tensor(out=xT[:, dc, t0:t0 + tn], in0=xT[:, dc, t0:t0 + tn], in1=sig[sb_][:, 0:tn], op=ALU.add), reads=[SIG[sb_], X[dc][tt]], writes=[X[dc][tt]])

            arF.reset()
            yst_f, _yt = arF.alloc(4096, 4)
            yst_t = yst_f.rearrange("p (b n) -> p b n", b=4)
            YST = list(_yt)
            yst_ds = xst_ds

            def out_tiles(tt):
                idx = [4 * tt + j for j in range(4)] if tt < 4 else [16]
                for i in idx:
                    t0, tn = TOK128[i]
                    b = i % 4
                    for half in range(2):
                        bk = nextbank()

                        def f(e):
                            r = None
                            for j in range(4):
                                c = half * 4 + j
                                r = e.transpose(bank(bk)[0:tn, j * 128:(j + 1) * 128], xT[:, c, t0:t0 + tn], ident)
                            return r
                        S.op("pe", f, reads=[X[c][tt] for c in range(half * 4, half * 4 + 4)] + [TC], writes=[PB[bk]])
                        if half == 0:
                            S.op("dve", lambda e: e.tensor_copy(out=yst_t[0:tn, b, 0:512], in_=bank(bk)[0:tn, 0:512]), reads=[PB[bk]], writes=[YST[b]])
                        else:
                            S.op("act", lambda e: e.activation(out=yst_t[0:tn, b, 512:1024], in_=bank(bk)[0:tn, 0:512], func=AF.Copy), reads=[PB[bk]], writes=[YST[b]])
                    store(y[t0:t0 + tn, :], yst_t[0:tn, b, :], [YST[b]], yst_ds[b])

            norm_to_h(64, out_fp32=True, after=out_tiles)

        except StopBuild:
            pass
        S.final_wait("sp", out_handles)
        S.run()
    return nc


_NC = [None]


def _consts():
    cf = np.zeros((128, NCF), np.float32)
    cf[:, 0:128] = np.eye(128, dtype=np.float32)
    sl = np.arange(128)[:, None]
    mk = np.zeros((128, 5, 128), np.float32)
    tl = np.arange(128)[None, :]
    mk[:, 0, :] = np.where((sl >= 64) & (tl < 64), -30000.0, 0.0)
    mk[:, 4, :] = np.where((sl < 64) & (tl >= 64), -30000.0, 0.0)
    cf[:, 128:768] = mk.reshape(128, 640)
    cb = np.zeros((128, NCB), np.float32)
    j = np.arange(128)[:, None]
    s = np.arange(128)[None, :]
    cb[:, 0:128] = np.where(j >= s, -1.0, 0.0)
    cb[:, 128:256] = -1.0
    cb[:, 256:384] = 1.0
    cb[:, 384:512] = np.where((j // 64) == (s // 64), 1.0, 0.0)
    t5 = np.arange(512)[None, :]
    for r in range(4):
        cb[:, 512 + r * 512:512 + (r + 1) * 512] = np.where(128 * r + sl < t5, 1.0, 0.0)
    cb[0:16, 2560:2576] = np.where(np.arange(16)[:, None] < np.arange(16)[None, :], 1.0, 0.0)
    cb[:, 2576:2704] = np.eye(128, dtype=np.float32)
    cb[:, 2704:2832] = np.where(j >= s, -30000.0, 0.0)
    return cf, cb


def kernel(x_prompt, x_sample, p_prompt, p_sample, cache_sb_k, cache_sb_v, cache_cb_k, cache_cb_v,
           g_mix, w_in, rel_table, g_out_sb, g_out_cb, w_out, g_ffn, w_gate, w_up, w_down,
           g_ple, w_ple_gate, w_ple_proj, g_final):
    f = lambda a: np.ascontiguousarray(np.asarray(a, dtype=np.float32))
    x_prompt, x_sample, p_prompt, p_sample = f(x_prompt), f(x_sample), f(p_prompt), f(p_sample)
    rel_table = f(rel_table)
    cf, cb = _consts()
    gpk = np.zeros((128, 72), np.float32)
    for l in range(2):
        gpk[:, 32 * l + 0:32 * l + 8] = f(g_mix)[l].reshape(8, 128).T
        gpk[:, 32 * l + 8:32 * l + 16] = f(g_ffn)[l].reshape(8, 128).T
        gpk[:, 32 * l + 16:32 * l + 24] = f(g_ple)[l].reshape(8, 128).T
        gpk[:, 32 * l + 24:32 * l + 28] = f(g_out_sb)[l].reshape(4, 128).T
        gpk[:, 32 * l + 28:32 * l + 32] = f(g_out_cb)[l].reshape(4, 128).T
    gpk[:, 64:72] = f(g_final).reshape(8, 128).T
    sl = np.arange(128)[:, None, None]
    o = np.arange(5)[None, :, None]
    tl = np.arange(128)[None, None, :]
    idxP = np.clip(128 * o + tl - sl, -128, 128) + 128
    bP = rel_table[:, idxP, :]
    bP = np.ascontiguousarray(np.transpose(bP, (0, 4, 1, 2, 3))).reshape(2, 8, 128, 640)
    r = np.arange(128)[:, None, None]
    blk = np.arange(4)[None, :, None]
    i16 = np.arange(16)[None, None, :]
    idxS = np.clip(512 - (blk * 128 + r) + i16, -128, 128) + 128
    bS = np.zeros((2, 8, 128, 80), np.float32)
    bS[:, :, :, 0:64] = np.transpose(rel_table[:, idxS, :], (0, 4, 1, 2, 3)).reshape(2, 8, 128, 64)
    jn = np.arange(16)[:, None]
    idxN = np.clip(np.arange(16)[None, :] - jn, -128, 128) + 128
    bS[:, :, 0:16, 64:80] = np.transpose(rel_table[:, idxN, :], (0, 3, 1, 2))
    bS = np.ascontiguousarray(bS.reshape(2, 4, 2, 128, 80).transpose(0, 1, 3, 2, 4)).reshape(2, 4, 128, 160)

    shared = {
        "w_in": f(w_in), "w_out": f(w_out), "w_gate": f(w_gate), "w_up": f(w_up), "w_down": f(w_down),
        "w_pg": f(w_ple_gate), "w_pp": f(w_ple_proj), "biasP": bP, "biasS": bS, "gpack": gpk,
        "constF": cf, "constB": cb,
    }
    csk, csv, cck, ccv = f(cache_sb_k), f(cache_sb_v), f(cache_cb_k), f(cache_cb_v)
    in_maps = []
    for c in range(8):
        m = dict(shared)
        m["xin"] = np.ascontiguousarray(np.concatenate([x_prompt[c], x_sample[c]], axis=0))
        m["pin"] = np.ascontiguousarray(np.concatenate([p_prompt[:, c], p_sample[:, c]], axis=1))
        m["csk"] = np.ascontiguousarray(csk[:, c].reshape(2, 2048, 512))
        m["csv"] = np.ascontiguousarray(csv[:, c].reshape(2, 2048, 512))
        m["cck"] = np.ascontiguousarray(cck[:, c].reshape(2, 512, 512))
        m["ccv"] = np.ascontiguousarray(ccv[:, c].reshape(2, 512, 512))
        in_maps.append(m)
    if _NC[0] is None:
        _NC[0] = build()
    res = run_bass_kernel_spmd(_NC[0], in_maps, core_ids=list(range(8)))
    R = res.results
    yo = np.stack([R[c]["y"] for c in range(8)])
    sbko = np.stack([R[c]["sbk"] for c in range(8)], axis=1)
    sbvo = np.stack([R[c]["sbv"] for c in range(8)], axis=1)
    cbko = np.stack([R[c]["cbk"] for c in range(8)], axis=1)
    cbvo = np.stack([R[c]["cbv"] for c in range(8)], axis=1)
    h4 = lambda a: np.ascontiguousarray(a).reshape(a.shape[0], a.shape[1], a.shape[2], 8, 64)
    return (np.ascontiguousarray(yo[:, :2048]), np.ascontiguousarray(yo[:, 2048:]),
            h4(sbko[:, :, :2048]), h4(sbvo[:, :, :2048]),
            h4(cbko[:, :, :512]), h4(cbvo[:, :, :512]),
            h4(sbko[:, :, 2048:]), h4(sbvo[:, :, 2048:]),
            h4(cbko[:, :, 512:]), h4(cbvo[:, :, 512:]))
```

```python
import numpy as np
import concourse.bass as bass
import concourse.mybir as mybir
from concourse.bass_utils import run_bass_kernel_spmd
from contextlib import ExitStack

F32 = mybir.dt.float32
BF16 = mybir.dt.bfloat16
AF = mybir.ActivationFunctionType
ALU = mybir.AluOpType

T = 2064
TT = [(0, 512), (512, 512), (1024, 512), (1536, 512), (2048, 16)]
TOK128 = [(i * 128, 128) for i in range(16)] + [(2048, 16)]
EPS = 1e-6
NFF = 22


import types


def freeze(fn, depth=0):
    if not isinstance(fn, types.FunctionType) or fn.__closure__ is None or depth > 4:
        return fn
    cells = []
    for c in fn.__closure__:
        try:
            v = c.cell_contents
        except ValueError:
            cells.append(c)
            continue
        if isinstance(v, types.FunctionType) and v.__closure__ is not None:
            v = freeze(v, depth + 1)
        cells.append(types.CellType(v))
    g = types.FunctionType(fn.__code__, fn.__globals__, fn.__name__, fn.__defaults__, tuple(cells))
    g.__kwdefaults__ = fn.__kwdefaults__
    return g


class Tok:
    __slots__ = ("lw", "rd", "name", "excl", "lw_read")

    def __init__(self, name="", dead=None, excl=False):
        self.lw = None
        self.rd = list(dead) if dead else []
        self.name = name
        self.excl = excl
        self.lw_read = False

    def handles(self):
        return ([self.lw] if self.lw is not None else []) + list(self.rd)


class DSem:
    def __init__(self, sem):
        self.sem = sem
        self.cnt = 0


class Sched:
    ENG = ["pe", "act", "dve", "pool", "sp"]
    LIM = 12000

    def __init__(self, nc, es):
        self.nc = nc
        self.es = es
        self.q = {e: [] for e in self.ENG}
        self.csem = {}
        self.ccnt = {}
        self.nsem = 0
        for e in ["pe", "act", "dve", "pool"]:
            self._newsem(e)
        self.waited = {e: {} for e in self.ENG}
        self.semobj = {}
        self.bg = None
        self._in_bg = False
        self.bg2 = None

    def _newsem(self, e):
        self.nsem += 1
        self.csem[e] = self.es.enter_context(self.nc.semaphore("c%s%d" % (e, self.nsem)))
        self.ccnt[e] = 0

    def dsem(self, name):
        self.nsem += 1
        return DSem(self.es.enter_context(self.nc.semaphore(name)))

    def op(self, eng, fn, reads=(), writes=(), dsem=None):
        fn = freeze(fn)
        is_dma = dsem is not None
        waits = {}

        def need(h, kind):
            sem, val, heng = h
            if (not is_dma) and heng == eng and kind != "raw":
                return
            key = id(sem)
            if key not in waits or waits[key][1] < val:
                waits[key] = (sem, val)

        for t in reads:
            if t.lw is not None:
                need(t.lw, "waw" if t.lw_read else "raw")
        for t in writes:
            if t.lw is not None:
                need(t.lw, "waw")
            for h in t.rd:
                need(h, "war")
        wl = []
        wd = self.waited[eng]
        for key, (sem, val) in waits.items():
            if wd.get(key, 0) >= val:
                continue
            wd[key] = val
            wl.append((sem, val))
        if is_dma:
            dsem.cnt += 16
            h = (dsem.sem, dsem.cnt, None)
            dsm = dsem.sem

            def emit(e, wl=wl, fn=fn, dsm=dsm):
                for s, val in wl:
                    e.wait_ge(s, val)
                fn(e).then_inc(dsm, 16)
        else:
            if self.ccnt[eng] >= self.LIM:
                self._newsem(eng)
            self.ccnt[eng] += 1
            sem = self.csem[eng]
            h = (sem, self.ccnt[eng], eng)

            def emit(e, wl=wl, fn=fn, sem=sem):
                for s, val in wl:
                    e.wait_ge(s, val)
                fn(e).then_inc(sem, 1)

        self.q[eng].append(emit)
        self._post = True
        for t in reads:
            if t.excl:
                t.lw = h
                t.lw_read = True
                continue
            if h[2] is not None:
                t.rd = [x for x in t.rd if not (x[2] == h[2] and x[0] is h[0])]
            t.rd.append(h)
        for t in writes:
            t.lw = h
            t.lw_read = False
            t.rd = []
        if self.bg is not None and not self._in_bg:
            self._step_bg()
        return h

    def _step_bg(self):
        self._in_bg = True
        try:
            next(self.bg)
        except StopIteration:
            self.bg = None
        self._in_bg = False
        self.bg2 = None

    def drain_bg(self):
        while self.bg is not None:
            self._step_bg()

    def tick(self):
        if self.bg2 is None:
            return False
        sv, self._in_bg = self._in_bg, True
        try:
            next(self.bg2)
            ok = True
        except StopIteration:
            self.bg2 = None
            ok = False
        self._in_bg = sv
        return ok

    def drain_bg2(self):
        while self.tick():
            pass

    def final_wait(self, eng, handles):
        best = {}
        for (sem, val, _) in handles:
            k = id(sem)
            if k not in best or best[k][1] < val:
                best[k] = (sem, val)
        wl = list(best.values())

        def emit(e, wl=wl):
            for sem, val in wl:
                e.wait_ge(sem, val)

        self.q[eng].append(emit)

    def run(self):
        nc = self.nc
        q = self.q
        with nc.Block() as block:
            @block.tensor
            def _(e):
                for f in q["pe"]:
                    f(e)

            @block.scalar
            def _(e):
                for f in q["act"]:
                    f(e)

            @block.vector
            def _(e):
                for f in q["dve"]:
                    f(e)

            @block.gpsimd
            def _(e):
                for f in q["pool"]:
                    f(e)

            @block.sync
            def _(e):
                for f in q["sp"]:
                    f(e)


class Arena:
    def __init__(self, tens, n):
        self.t = tens
        self.n = n
        self.off = 0
        self.live = []
        self.dead = []

    def reset(self):
        hs = []
        for t in self.live:
            hs.extend(t.handles())
        hs.extend(self.dead)
        best = {}
        for h in hs:
            k = id(h[0])
            if k not in best or best[k][1] < h[1]:
                best[k] = h
        self.dead = list(best.values())
        self.live = []
        self.off = 0

    def alloc(self, n, ntok=1):
        n_al = (n + 15) // 16 * 16
        assert self.off + n_al <= self.n, ("arena overflow", self.off, n_al, self.n)
        ap = self.t[:, self.off:self.off + n]
        self.off += n_al
        toks = [Tok(dead=self.dead) for _ in range(ntok)]
        self.live.extend(toks)
        return ap, (toks[0] if ntok == 1 else toks)


import os
STOP = int(os.environ.get("KSTOP", "0"))


class StopBuild(Exception):
    pass


def stop_at(k):
    if STOP == k:
        raise StopBuild()


NCF = 128 + 640
NCB = 4 * 128 + 4 * 512 + 16 + 256


def build():
    nc = bass.Bass("TRN2", target_bir_lowering=False)

    def din(name, shape):
        return nc.dram_tensor(name, list(shape), F32, kind="ExternalInput").ap()

    def dout(name, shape):
        return nc.dram_tensor(name, list(shape), F32, kind="ExternalOutput").ap()

    xin = din("xin", [T, 1024])
    pin = din("pin", [2, T, 256])
    csk = din("csk", [2, 2048, 512])
    csv = din("csv", [2, 2048, 512])
    cck = din("cck", [2, 512, 512])
    ccv = din("ccv", [2, 512, 512])
    w_in = din("w_in", [2, 1024, 3072])
    w_out = din("w_out", [2, 1024, 1024])
    w_gate = din("w_gate", [2, 1024, 2816])
    w_up = din("w_up", [2, 1024, 2816])
    w_down = din("w_down", [2, 2816, 1024])
    w_pg = din("w_pg", [2, 1024, 1024])
    w_pp = din("w_pp", [2, 256, 1024])
    biasP = din("biasP", [2, 8, 128, 640])
    biasS = din("biasS", [2, 4, 128, 160])
    gpack = din("gpack", [128, 72])
    constF = din("constF", [128, NCF])
    constB = din("constB", [128, NCB])
    y = dout("y", [T, 1024])
    sbk = dout("sbk", [2, T, 512])
    sbv = dout("sbv", [2, T, 512])
    cbk = dout("cbk", [2, 528, 512])
    cbv = dout("cbv", [2, 528, 512])

    with ExitStack() as es:
        S = Sched(nc, es)

        def sb(name, shape, dt):
            return es.enter_context(nc.sbuf_tensor(name, shape, dt))

        xT = sb("xT", [128, 8, T], F32)
        hT = sb("hT", [128, 8, T], BF16)
        X = [[Tok() for _ in TT] for _ in range(8)]
        H = [[Tok() for _ in TT] for _ in range(8)]
        cF = sb("cF", [128, NCF], F32)
        cB = sb("cB", [128, NCB], BF16)
        gp = sb("gp", [128, 72], F32)
        TC = Tok()
        ident = cF[:, 0:128]
        maskc = cF[:, 128:768]
        negTri = cB[:, 0:128]
        negOnes = cB[:, 128:256]
        ones = cB[:, 256:384]
        blk64 = cB[:, 384:512]
        Mr = [cB[:, 512 + r * 512:512 + (r + 1) * 512] for r in range(4)]
        M16 = cB[0:16, 2560:2576]
        identB = cB[:, 2576:2704]
        NEGM = cB[:, 2704:2832]
        NA_B = 34208
        NA_F = 6304
        arB_t = sb("arB", [128, NA_B], BF16)
        arF_t = sb("arF", [128, NA_F], F32)
        arB = Arena(arB_t, NA_B)
        arF = Arena(arF_t, NA_F)
        sq_t = sb("sq", [128, 2, 512], BF16)
        SQ = [Tok(), Tok()]
        rstd_t = sb("rstd", [128, 2, 512], F32)
        RSTD = [Tok(), Tok()]
        stg_t = sb("stg", [128, 3, 256], F32)
        STG = [Tok(), Tok(), Tok()]
        stg_ds = [S.dsem("stg%d" % i) for i in range(3)]
        stg_i = [0]
        pp = [es.enter_context(nc.psum_tensor("pp%d" % i, [128, 1024], F32)) for i in range(4)]
        PB = [Tok(excl=True) for _ in range(8)]

        def bank(i):
            return pp[i // 2][:, (i % 2) * 512:(i % 2) * 512 + 512]

        bank_rr = [0]

        def nextbank(lo=0, hi=8):
            b = lo + bank_rr[0] % (hi - lo)
            bank_rr[0] += 1
            return b

        ld_ds = [S.dsem("ld%d" % i) for i in range(2)]
        ld_i = [0]

        def load(eng, out_ap, in_ap, wtoks, ds=None):
            if ds is None:
                ds = ld_ds[ld_i[0] % len(ld_ds)]
                ld_i[0] += 1
            S.op(eng, lambda e, o=out_ap, i=in_ap: e.dma_start(out=o, in_=i), writes=wtoks, dsem=ds)

        out_handles = []

        def store(out_ap, in_ap, rtoks, ds):
            h = S.op("sp", lambda e, o=out_ap, i=in_ap: e.dma_start(out=o, in_=i), reads=rtoks, dsem=ds)
            out_handles.append(h)

        cds = [S.dsem("cst%d" % i) for i in range(3)]
        load("sp", cF[:], constF[:, :], [TC], cds[0])
        load("sp", gp[:], gpack[:, :], [TC], cds[1])
        load("pool", cB[:], constB[:, :], [TC], cds[2])

        xst_f, _xt = arF.alloc(4096, 4)
        xst_t = xst_f.rearrange("p (b n) -> p b n", b=4)
        XST = list(_xt)
        xst_ds = [S.dsem("xst%d" % i_) for i_ in range(4)]
        for i, (t0, tn) in enumerate(TOK128):
            b = i % 4
            tt = min(t0 // 512, 4)
            load("sp", xst_t[0:tn, b, :], xin[t0:t0 + tn, :], [XST[b]], xst_ds[b])
            for half in range(2):
                bk = nextbank()

                def f(e, tn=tn, b=b, half=half, bk=bk):
                    r = None
                    for j in range(4):
                        c = half * 4 + j
                        r = e.transpose(bank(bk)[:, j * 128:j * 128 + tn], xst_t[0:tn, b, c * 128:(c + 1) * 128], ident[0:tn, 0:tn])
                    return r
                S.op("pe", f, reads=[XST[b], TC], writes=[PB[bk]])
                src = bank(bk).rearrange("p (c t) -> p c t", c=4)[:, :, 0:tn]
                dst = xT[:, half * 4:half * 4 + 4, t0:t0 + tn]
                eng = "dve" if half == 0 else "act"
                if eng == "dve":
                    S.op("dve", lambda e, d=dst, s=src: e.tensor_copy(out=d, in_=s), reads=[PB[bk]], writes=[X[c][tt] for c in range(half * 4, half * 4 + 4)])
                else:
                    S.op("act", lambda e, d=dst, s=src: e.activation(out=d, in_=s, func=AF.Copy), reads=[PB[bk]], writes=[X[c][tt] for c in range(half * 4, half * 4 + 4)])

        def rstd_from_bank(bk, n, scale, k):
            S.op("act", lambda e: e.activation(out=rstd_t[:, k, 0:n], in_=bank(bk)[:, 0:n], func=AF.Ln, bias=EPS, scale=scale), reads=[PB[bk]], writes=[RSTD[k]])
            S.op("act", lambda e: e.activation(out=rstd_t[:, k, 0:n], in_=rstd_t[:, k, 0:n], func=AF.Exp, scale=-0.5), reads=[RSTD[k]], writes=[RSTD[k]])

        norm_i = [0]

        def norm_to_h(gcol, out_fp32=False, after=None):
            ks = {}

            def pa(tt):
                t0, tn = TT[tt]
                bk = nextbank()
                k = norm_i[0] % 2
                norm_i[0] += 1
                ks[tt] = k
                for c in range(8):
                    s_ = c % 2
                    S.op("act", lambda e: e.activation(out=sq_t[:, s_, 0:tn], in_=xT[:, c, t0:t0 + tn], func=AF.Square), reads=[X[c][tt]], writes=[SQ[s_]])
                    S.op("pe", lambda e: e.matmul(bank(bk)[:, 0:tn], lhsT=ones, rhs=sq_t[:, s_, 0:tn], start=(c == 0), stop=(c == 7)), reads=[SQ[s_], TC], writes=[PB[bk]])
                rstd_from_bank(bk, tn, 1.0 / 1024, k)

            def pb(tt):
                t0, tn = TT[tt]
                k = ks[tt]
                for c in range(8):
                    if out_fp32:
                        S.op("dve", lambda e: e.scalar_tensor_tensor(out=xT[:, c, t0:t0 + tn], in0=xT[:, c, t0:t0 + tn], scalar=gp[:, gcol + c:gcol + c + 1], in1=rstd_t[:, k, 0:tn], op0=ALU.mult, op1=ALU.mult),
                             reads=[X[c][tt], RSTD[k], TC], writes=[X[c][tt]])
                    else:
                        S.op("dve", lambda e: e.scalar_tensor_tensor(out=hT[:, c, t0:t0 + tn], in0=xT[:, c, t0:t0 + tn], scalar=gp[:, gcol + c:gcol + c + 1], in1=rstd_t[:, k, 0:tn], op0=ALU.mult, op1=ALU.mult),
                             reads=[X[c][tt], RSTD[k], TC], writes=[H[c][tt]])

            pa(0)
            for tt in range(len(TT)):
                if tt + 1 < len(TT):
                    pa(tt + 1)
                pb(tt)
                if after is not None:
                    after(tt)

        def add_to_x(bk, dc, tt, eng="dve"):
            t0, tn = TT[tt]
            S.op(eng, lambda e: e.tensor_tensor(out=xT[:, dc, t0:t0 + tn], in0=xT[:, dc, t0:t0 + tn], in1=bank(bk)[:, 0:tn], op=ALU.add), reads=[PB[bk], X[dc][tt]], writes=[X[dc][tt]])

        wsl_ds = [S.dsem("wsl0"), S.dsem("wsl1")]
        wo_ds = [S.dsem("wo0"), S.dsem("wo1")]
        vc_ds = S.dsem("vc")
        kc_ds = [S.dsem("kc0"), S.dsem("kc1")]
        eb_ds = S.dsem("eb")
        ebs_ds = S.dsem("ebs")
        ring_ds = [[S.dsem("rg%d_%d" % (s_, j_)) for j_ in range(3)] for s_ in range(2)]
        pst_ds = [S.dsem("pst0"), S.dsem("pst1")]
        wg_ds = S.dsem("wg")
        wp_ds = S.dsem("wp")
        try:
            for l in range(2):
                g0 = 32 * l
                stop_at(1)
                for g in range(2):
                    arB.reset()
                    arF.reset()
                    wsl = []
                    WSL = []
                    for b in range(2):
                        ap, tk = arB.alloc(8 * 3 * 128)
                        wsl.append(ap.rearrange("p (c j n) -> p c j n", c=8, j=3))
                        WSL.append(tk)
                    wo = []
                    WO = []
                    for b in range(2):
                        ap, tk = arB.alloc(1024)
                        wo.append(ap)
                        WO.append(tk)
                    qPb, Qb, kPb, Kb, vPb, Vb = [], [], [], [], [], []
                    for b_ in range(2):
                        ap, tk = arB.alloc(T, 5)
                        qPb.append(ap)
                        Qb.append(tk)
                        ap, tk = arB.alloc(T, 5)
                        kPb.append(ap)
                        Kb.append(tk)
                        ap, tk = arB.alloc(17 * 128, 17)
                        vPb.append(ap.rearrange("p (i n) -> p i n", i=17))
                        Vb.append(tk)
                    mixP, MX = arB.alloc(T, 5)
                    kcT, KCT = arB.alloc(2048, 4)
                    vc_f, VC = arB.alloc(16 * 128)
                    vc = vc_f.rearrange("p (k n) -> p k n", k=16)
                    kc = []
                    KC = []
                    for b in range(2):
                        ap, tk = arF.alloc(512)
                        kc.append(ap.rearrange("p (j n) -> p j n", j=4))
                        KC.append(tk)
                    if g == 0:
                        osb, OSB = arF.alloc(512)
                    oss, OSS = arF.alloc(16)
                    sqh, SQH = arB.alloc(512)
                    if g == 0:
                        e_t = []
                        E = []
                        for b in range(2):
                            ap, tk = arF.alloc(1024)
                            e_t.append(ap)
                            E.append(tk)
                        Lp = []
                        LP = []
                        wbf = []
                        WB = []
                        for b in range(2):
                            ap, tk = arB.alloc(1024)
                            Lp.append(ap)
                            LP.append(tk)
                            ap, tk = arB.alloc(1024)
                            wbf.append(ap)
                            WB.append(tk)
                        LaccB, LACC = arB.alloc(1024)
                        es_f, ES = arF.alloc(2 * 272)
                        es_t = es_f.rearrange("p (h c) -> p h c", h=2)
                        Ls_f, LS = arB.alloc(2 * 272)
                        Ls = Ls_f.rearrange("p (h c) -> p h c", h=2)
                        suf_f, SUF = arB.alloc(2 * 256)
                        suf = suf_f.rearrange("p (h c) -> p h c", h=2)
                        ws_f, WS = arB.alloc(2 * 272)
                        ws = ws_f.rearrange("p (h c) -> p h c", h=2)
                    else:
                        EB_f, EB = arF.alloc(2 * 640)
                        EBt = EB_f.rearrange("p (h c) -> p h c", h=2)
                        bh_f, BH = arB.alloc(2 * 640)
                        biasH = bh_f.rearrange("p (h c) -> p h c", h=2)
                        bl_f, BL = arB.alloc(2 * 640)
                        biasL = bl_f.rearrange("p (h c) -> p h c", h=2)
                        pcb = []
                        PCB = []
                        for b_ in range(2):
                            ap, tk = arB.alloc(2 * 640)
                            pcb.append(ap.rearrange("p (h c) -> p h c", h=2))
                            PCB.append(tk)
                        od_l = []
                        OSB_l = []
                        for b_ in range(2):
                            ap, tk = arF.alloc(1024)
                            od_l.append(ap.rearrange("p (a c) -> p a c", a=2))
                            OSB_l.append(tk)
                        EBS_f, EBSK = arF.alloc(160)
                        EBS = EBS_f.rearrange("p (h c) -> p h c", h=2)
                        ecs_f, ECS = arF.alloc(160)
                        ecs = ecs_f.rearrange("p (h c) -> p h c", h=2)
                        pcs_f, PCS = arB.alloc(160)
                        pcs = pcs_f.rearrange("p (h c) -> p h c", h=2)
                        recs, RECS = arF.alloc(16)

                    def head_norm(src, SRC, t0, n, gcol, tt, bk=None):
                        S.op("dve", lambda e: e.tensor_tensor(out=sqh[:, 0:n], in0=src[:, 0:n], in1=src[:, 0:n], op=ALU.mult), reads=[SRC], writes=[SQH])
                        if bk is None:
                            bk = nextbank(6, 8)
                        S.op("pe", lambda e: e.matmul(bank(bk)[:, 0:n], lhsT=blk64, rhs=sqh[:, 0:n], start=True, stop=True), reads=[SQH, TC], writes=[PB[bk]])
                        k = norm_i[0] % 2
                        norm_i[0] += 1
                        rstd_from_bank(bk, n, 1.0 / 64, k)
                        S.op("dve", lambda e: e.scalar_tensor_tensor(out=mixP[:, t0:t0 + n], in0=src[:, 0:n], scalar=gp[:, gcol:gcol + 1], in1=rstd_t[:, k, 0:n], op0=ALU.mult, op1=ALU.mult),
                             reads=[SRC, RSTD[k], TC], writes=[MX[tt]])

                    def issue_loads(pq):
                        b = pq % 2
                        for j in range(3):
                            c0 = g * 1536 + j * 512 + pq * 128
                            src = w_in[l, :, c0:c0 + 128].rearrange("(c p) n -> p c n", p=128)
                            load("pool", wsl[b][:, :, j, :], src, [WSL[b]], wsl_ds[b])
                        load("pool", wo[b], w_out[l, (g * 4 + pq) * 128:(g * 4 + pq + 1) * 128, :], [WO[b]], wo_ds[b])

                    def qkv_gen(pn, banks, ev_v):
                        bb = pn % 2
                        qPn, kPn, vPn, Qn, Kn, Vn = qPb[bb], kPb[bb], vPb[bb], Qb[bb], Kb[bb], Vb[bb]
                        bi = [0]

                        def nb_():
                            x_ = banks[bi[0] % len(banks)]
                            bi[0] += 1
                            return x_
                        def fm(tt):
                            t0, tn = TT[tt]
                            for j in range(2):
                                bk = nb_()

                                def f(e):
                                    r = None
                                    for c in range(8):
                                        r = e.matmul(bank(bk)[:, 0:tn], lhsT=wsl[bb][:, c, j, :], rhs=hT[:, c, t0:t0 + tn], start=(c == 0), stop=(c == 7))
                                    return r
                                S.op("pe", f, reads=[WSL[bb]] + [H[c][tt] for c in range(8)], writes=[PB[bk]])
                                if j == 0:
                                    S.op("act", lambda e: e.activation(out=qPn[:, t0:t0 + tn], in_=bank(bk)[:, 0:tn], func=AF.Copy, scale=0.125), reads=[PB[bk]], writes=[Qn[tt]])
                                else:
                                    S.op("dve", lambda e: e.tensor_copy(out=kPn[:, t0:t0 + tn], in_=bank(bk)[:, 0:tn]), reads=[PB[bk]], writes=[Kn[tt]])
                                yield
                        def tm(i):
                            t0, tn = TOK128[i]
                            tt = min(t0 // 512, 4)
                            want_out = (g == 0) or (i >= 12)
                            bk = nb_()
                            if want_out:
                                def f(e):
                                    r = None
                                    for c in range(8):
                                        r = e.matmul(bank(bk)[0:tn, 0:256], lhsT=hT[:, c, t0:t0 + tn], rhs=wsl[bb][:, c, 1:3, :], start=(c == 0), stop=(c == 7))
                                    return r
                                voff = 128
                            else:
                                def f(e):
                                    r = None
                                    for c in range(8):
                                        r = e.matmul(bank(bk)[0:tn, 0:128], lhsT=hT[:, c, t0:t0 + tn], rhs=wsl[bb][:, c, 2, :], start=(c == 0), stop=(c == 7))
                                    return r
                                voff = 0
                            S.op("pe", f, reads=[WSL[bb]] + [H[c][tt] for c in range(8)], writes=[PB[bk]])
                            if ev_v == "act":
                                S.op("act", lambda e: e.activation(out=vPn[0:tn, i, :], in_=bank(bk)[0:tn, voff:voff + 128], func=AF.Copy), reads=[PB[bk]], writes=[Vn[i]])
                            else:
                                S.op("dve", lambda e: e.tensor_copy(out=vPn[0:tn, i, :], in_=bank(bk)[0:tn, voff:voff + 128]), reads=[PB[bk]], writes=[Vn[i]])
                            if want_out:
                                si = stg_i[0] % 3
                                stg_i[0] += 1
                                S.op("dve", lambda e: e.tensor_copy(out=stg_t[0:tn, si, 0:256], in_=bank(bk)[0:tn, 0:256]), reads=[PB[bk]], writes=[STG[si]])
                                if g == 0:
                                    ko = sbk[l, t0:t0 + tn, pn * 128:(pn + 1) * 128]
                                    vo = sbv[l, t0:t0 + tn, pn * 128:(pn + 1) * 128]
                                else:
                                    r0 = t0 - 1536
                                    ko = cbk[l, r0:r0 + tn, pn * 128:(pn + 1) * 128]
                                    vo = cbv[l, r0:r0 + tn, pn * 128:(pn + 1) * 128]
                                store(ko, stg_t[0:tn, si, 0:128], [STG[si]], stg_ds[si])
                                store(vo, stg_t[0:tn, si, 128:256], [STG[si]], stg_ds[si])
                            yield

                        for tt in range(5):
                            yield from fm(tt)
                            for i in (range(4 * tt, 4 * tt + 4) if tt < 4 else [16]):
                                yield from tm(i)

                    issue_loads(0)
                    issue_loads(1)
                    if g == 0:
                        S.bg2 = qkv_gen(0, list(range(8)), "act")
                        norm_to_h(g0 + 0, after=lambda tt: [S.tick() for _ in range(2 + (4 if tt < 4 else 1))])
                        S.drain_bg2()
                    else:
                        for _ in qkv_gen(0, list(range(8)), "act"):
                            pass
                    for pq in range(4):
                        b = pq % 2
                        qP, kP, vP, Q, K, V = qPb[b], kPb[b], vPb[b], Qb[b], Kb[b], Vb[b]
                        gcol_o = g0 + 24 + g * 4 + pq
                        if pq + 1 < 4:
                            S.bg2 = qkv_gen(pq + 1, [3] if g == 0 else [0, 1, 2, 3, 6, 7], "dve" if g == 0 else "act")
                        stop_at(2)
                        wo_i = [0]

                        def wout_gen(tts):
                            for tt in tts:
                                t0, tn = TT[tt]
                                for dc in range(8):
                                    bk = 4 + (wo_i[0] % 2)
                                    wo_i[0] += 1
                                    S.op("pe", lambda e: e.matmul(bank(bk)[:, 0:tn], lhsT=wo[b][:, dc * 128:(dc + 1) * 128], rhs=mixP[:, t0:t0 + tn], start=True, stop=True), reads=[WO[b], MX[tt]], writes=[PB[bk]])
                                    add_to_x(bk, dc, tt)
                                    yield

                        if g == 0:
                            steps = []
                            for qi in range(4):
                                for kb in range(4 * qi + 3, -1, -1):
                                    steps.append((qi, kb))

                            Mtri = Mr[0][:, 0:128]
                            zv = [pp[0].rearrange("p (h c) -> p h c", h=2), pp[0].rearrange("p (h c) -> p h c", h=2)]
                            osets = [(pp[3], 6), (pp[3], 6)]
                            av = pp[2].rearrange("p (h c) -> p h c", h=2)
                            ev = [x_.rearrange("p (h c) -> p h c", h=2) for x_ in e_t]
                            lv = [x_.rearrange("p (h c) -> p h c", h=2) for x_ in Lp]
                            wv = [x_.rearrange("p (h c) -> p h c", h=2) for x_ in wbf]
                            lacv = LaccB.rearrange("p (h c) -> p h c", h=2)

                            def geom(i):
                                qi, kb = steps[i]
                                r = kb - 4 * qi
                                c0 = 128 * r if r >= 0 else 0
                                return qi, kb, r, c0, i % 2, qi * 512

                            def s1a(i):
                                qi, kb, r, c0, zi, t0 = geom(i)

                                def f(e):
                                    dg = (r >= 0)
                                    e.matmul(zv[zi][:, 0, c0:512], lhsT=kP[0:64, kb * 128:(kb + 1) * 128], rhs=qP[0:64, t0 + c0:t0 + 512], start=True, stop=not dg, skip_group_check=True)
                                    rr = e.matmul(zv[zi][:, 1, c0:512], lhsT=kP[64:128, kb * 128:(kb + 1) * 128], rhs=qP[64:128, t0 + c0:t0 + 512], start=True, stop=not dg, skip_group_check=True)
                                    if dg:
                                        for h in range(2):
                                            rr = e.matmul(zv[zi][:, h, c0:c0 + 128], lhsT=identB, rhs=NEGM, start=False, stop=True, skip_group_check=True)
                                    return rr
                                S.op("pe", f, reads=[K[kb // 4], Q[qi], TC], writes=[PB[0], PB[1]])
                                S.op("act", lambda e: e.activation(out=ev[zi][:, :, c0:512], in_=zv[zi][:, :, c0:512], func=AF.Exp), reads=[PB[0], PB[1]], writes=[E[zi]])

                            def s1b(i):
                                qi, kb, r, c0, zi, t0 = geom(i)
                                S.op("act", lambda e: e.activation(out=lv[zi][:, :, c0:512], in_=ev[zi][:, :, c0:512], func=AF.Ln, bias=1.0), reads=[E[zi]], writes=[LP[zi]])

                            def s2a(i):
                                qi, kb, r, c0, zi, t0 = geom(i)
                                first = (r == 3)
                                c1 = c0 + 128 if r >= 0 else 0

                                def f(e):
                                    rr = None
                                    for h in range(2):
                                        e.matmul(av[:, h, c0:512], lhsT=kP[64 * h:64 * h + 64, kb * 128:(kb + 1) * 128], rhs=qP[64 * h:64 * h + 64, t0 + c0:t0 + 512], start=True, stop=False, skip_group_check=True)
                                    if r >= 0:
                                        for h in range(2):
                                            e.matmul(av[:, h, c0:c0 + 128], lhsT=identB, rhs=NEGM, start=False, stop=False, skip_group_check=True)
                                    for h in range(2):
                                        rr = e.matmul(av[:, h, c0:512], lhsT=negTri, rhs=lv[zi][:, h, c0:512], start=False, stop=first, skip_group_check=True)
                                    return rr
                                S.op("pe", f, reads=[K[kb // 4], Q[qi], LP[zi], TC], writes=[PB[4], PB[5]])
                                if not first:
                                    def f2(e):
                                        rr = None
                                        for h in range(2):
                                            rr = e.matmul(av[:, h, c1:512], lhsT=negOnes, rhs=lacv[:, h, c1:512], start=False, stop=True, skip_group_check=True)
                                        return rr
                                    S.op("pe", f2, reads=[LACC, TC], writes=[PB[4], PB[5]])
                                S.op("act", lambda e: e.activation(out=wv[zi][:, :, c0:512], in_=av[:, :, c0:512], func=AF.Exp), reads=[PB[4], PB[5]], writes=[WB[zi]])

                            def s2b(i):
                                qi, kb, r, c0, zi, t0 = geom(i)
                                first = (r == 3)
                                c1 = c0 + 128 if r >= 0 else 0
                                if kb > 0:
                                    if r >= 0:
                                        S.op("dve", lambda e: e.tensor_copy(out=lacv[:, :, c0:c0 + 128], in_=lv[zi][:, :, c0:c0 + 128]), reads=[LP[zi]], writes=[LACC])
                                    if c1 < 512:
                                        S.op("dve", lambda e: e.tensor_tensor(out=lacv[:, :, c1:512], in0=lacv[:, :, c1:512], in1=lv[zi][:, :, c1:512], op=ALU.add), reads=[LP[zi], LACC], writes=[LACC])

                            def s2c(i):
                                qi, kb, r, c0, zi, t0 = geom(i)
                                first = (r == 3)

                                ot, ob0 = osets[qi % 2]

                                def f(e):
                                    e.matmul(ot[:, c0:512], lhsT=vP[:, kb, :], rhs=wv[zi][:, 0, c0:512], start=first, stop=(kb == 0), skip_group_check=True)
                                    return e.matmul(ot[:, 512 + c0:1024], lhsT=vP[:, kb, :], rhs=wv[zi][:, 1, c0:512], start=first, stop=(kb == 0), skip_group_check=True)
                                S.op("pe", f, reads=[V[kb], WB[zi]], writes=[PB[ob0], PB[ob0 + 1]])
                                if kb == 0:
                                    S.op("dve", lambda e: e.tensor_copy(out=osb[0:64, 0:512], in_=ot[0:64, 0:512]), reads=[PB[ob0]], writes=[OSB])
                                    S.op("dve", lambda e: e.tensor_copy(out=osb[64:128, 0:512], in_=ot[64:128, 512:1024]), reads=[PB[ob0 + 1]], writes=[OSB])
                                    pending.append([3, (t0, qi, ob0)])

                            NFILL = int(os.environ.get("KFILL", "2"))

                            def fill(i):
                                zi2 = i % 2

                                def f(e):
                                    rr = None
                                    for j in range(NFILL):
                                        rr = e.matmul(zv[zi2][:, j % 2, 0:512], lhsT=negTri, rhs=Mr[0], start=True, stop=True)
                                    return rr
                                S.op("pe", f, reads=[TC], writes=[PB[0], PB[1]])

                            n = len(steps)
                            pending = []
                            s1a(0)
                            s1b(0)
                            for i in range(n):
                                if i + 1 < n:
                                    s1a(i + 1)
                                s2a(i)
                                s2b(i)
                                if i >= 1:
                                    s2c(i - 1)
                                if i + 1 < n:
                                    s1b(i + 1)
                                if i >= 1 and S.tick():
                                    pass
                                elif NFILL and 1 <= i < n - 2:
                                    fill(i)
                                for p_ in list(pending):
                                    p_[0] -= 1
                                    if p_[0] <= 0:
                                        pending.remove(p_)
                                        head_norm(osb, OSB, p_[1][0], 512, gcol_o, p_[1][1], bk=2)
                            s2c(n - 1)
                            for p_ in pending:
                                head_norm(osb, OSB, p_[1][0], 512, gcol_o, p_[1][1], bk=2)
                            S.drain_bg2()

                            stop_at(3)
                            S.bg = wout_gen([0, 1, 2, 3])
                            load("pool", vc, csv[l, :, pq * 128:(pq + 1) * 128].rearrange("(k s) n -> s k n", s=128), [VC], vc_ds)
                            for cq in range(4):
                                kb_ = cq % 2
                                load("sp", kc[kb_], csk[l, cq * 512:(cq + 1) * 512, pq * 128:(pq + 1) * 128].rearrange("(j s) n -> s j n", s=128), [KC[kb_]], kc_ds[kb_])
                                bk = nextbank(6, 8)

                                def f(e, kb_=kb_, bk=bk):
                                    r = None
                                    for j in range(4):
                                        r = e.transpose(bank(bk)[:, j * 128:(j + 1) * 128], kc[kb_][:, j, :], ident)
                                    return r
                                S.op("pe", f, reads=[KC[kb_], TC], writes=[PB[bk]])
                                S.op("act", lambda e, cq=cq, bk=bk: e.activation(out=kcT[:, cq * 512:(cq + 1) * 512], in_=bank(bk)[:, 0:512], func=AF.Copy), reads=[PB[bk]], writes=[KCT[cq]])

                            zsv = pp[0].rearrange("p (h c) -> p h c", h=2)
                            asv = pp[1].rearrange("p (h c) -> p h c", h=2)

                            def zmm(e, dst, stop):
                                r = None
                                for h in range(2):
                                    for kb in range(16):
                                        st = True if stop else (kb == 0)
                                        e.matmul(dst[:, h, kb * 16:(kb + 1) * 16], lhsT=kcT[64 * h:64 * h + 64, kb * 128:(kb + 1) * 128], rhs=qP[64 * h:64 * h + 64, 2048:2064], start=st, stop=stop, skip_group_check=True)
                                    r = e.matmul(dst[0:16, h, 256:272], lhsT=kP[64 * h:64 * h + 64, 2048:2064], rhs=qP[64 * h:64 * h + 64, 2048:2064], start=bool(stop), stop=stop, skip_group_check=True)
                                return r
                            S.op("pe", lambda e: zmm(e, zsv, True), reads=KCT + [K[4], Q[4]], writes=[PB[0], PB[1]])
                            S.op("act", lambda e: e.activation(out=es_t[:, :, 0:256], in_=zsv[:, :, 0:256], func=AF.Exp), reads=[PB[0], PB[1]], writes=[ES])
                            S.op("act", lambda e: e.activation(out=es_t[0:16, :, 256:272], in_=zsv[0:16, :, 256:272], func=AF.Exp), reads=[PB[0], PB[1]], writes=[ES])
                            S.op("act", lambda e: e.activation(out=Ls[:, :, 0:256], in_=es_t[:, :, 0:256], func=AF.Ln, bias=1.0), reads=[ES], writes=[LS])
                            S.op("act", lambda e: e.activation(out=Ls[0:16, :, 256:272], in_=es_t[0:16, :, 256:272], func=AF.Ln, bias=1.0), reads=[ES], writes=[LS])
                            for h in range(2):
                                S.op("pool", lambda e, h=h: e.tensor_tensor(out=Ls[0:16, h, 256:272], in0=Ls[0:16, h, 256:272], in1=M16, op=ALU.mult), reads=[LS, TC], writes=[LS])
                            S.op("pool", lambda e: e.memset(suf[:, :, 240:256], 0.0), writes=[SUF])
                            for kb in range(14, -1, -1):
                                S.op("pool", lambda e, kb=kb: e.tensor_tensor(out=suf[:, :, kb * 16:(kb + 1) * 16], in0=suf[:, :, (kb + 1) * 16:(kb + 2) * 16], in1=Ls[:, :, (kb + 1) * 16:(kb + 2) * 16], op=ALU.add), reads=[SUF, LS], writes=[SUF])

                            def amm(e):
                                zmm(e, asv, False)
                                r = None
                                for h in range(2):
                                    e.matmul(asv[:, h, 0:256], lhsT=negTri, rhs=Ls[:, h, 0:256], start=False, stop=False, skip_group_check=True)
                                    e.matmul(asv[:, h, 0:256], lhsT=negOnes, rhs=suf[:, h, 0:256], start=False, stop=False, skip_group_check=True)
                                    for kb in range(16):
                                        e.matmul(asv[:, h, kb * 16:(kb + 1) * 16], lhsT=negOnes[0:16, :], rhs=Ls[0:16, h, 256:272], start=False, stop=True, skip_group_check=True)
                                    r = e.matmul(asv[0:16, h, 256:272], lhsT=negTri[0:16, 0:16], rhs=Ls[0:16, h, 256:272], start=False, stop=True, skip_group_check=True)
                                return r
                            S.op("pe", amm, reads=KCT + [K[4], Q[4], LS, SUF, TC], writes=[PB[2], PB[3]])
                            S.op("act", lambda e: e.activation(out=ws[:, :, 0:256], in_=asv[:, :, 0:256], func=AF.Exp), reads=[PB[2], PB[3]], writes=[WS])
                            S.op("act", lambda e: e.activation(out=ws[0:16, :, 256:272], in_=asv[0:16, :, 256:272], func=AF.Exp), reads=[PB[2], PB[3]], writes=[WS])
                            for h in range(2):
                                S.op("pool", lambda e, h=h: e.tensor_tensor(out=ws[0:16, h, 256:272], in0=ws[0:16, h, 256:272], in1=M16, op=ALU.mult), reads=[WS, TC], writes=[WS])

                            def avs(e):
                                r = None
                                for h in range(2):
                                    o = pp[3][:, h * 512:h * 512 + 16]
                                    for kb in range(16):
                                        e.matmul(o, lhsT=vc[:, kb, :], rhs=ws[:, h, kb * 16:(kb + 1) * 16], start=(kb == 0), stop=False)
                                    r = e.matmul(o, lhsT=vP[0:16, 16, :], rhs=ws[0:16, h, 256:272], start=False, stop=True)
                                return r
                            S.op("pe", avs, reads=[VC, V[16], WS], writes=[PB[6], PB[7]])
                            S.op("act", lambda e: e.activation(out=oss[0:64, 0:16], in_=pp[3][0:64, 0:16], func=AF.Copy), reads=[PB[6]], writes=[OSS])
                            S.op("act", lambda e: e.activation(out=oss[64:128, 0:16], in_=pp[3][64:128, 512:528], func=AF.Copy), reads=[PB[7]], writes=[OSS])
                            head_norm(oss, OSS, 2048, 16, gcol_o, 4)
                        else:
                            load("sp", EBt, biasP[l, 2 * pq:2 * pq + 2].rearrange("h p c -> p h c"), [EB], eb_ds)
                            for h in range(2):
                                S.op("dve", lambda e, h=h: e.tensor_tensor(out=EBt[:, h, :], in0=EBt[:, h, :], in1=maskc, op=ALU.add), reads=[EB, TC], writes=[EB])
                            S.op("dve", lambda e: e.tensor_copy(out=biasH[:, :, :], in_=EBt[:, :, :]), reads=[EB], writes=[BH])
                            S.op("dve", lambda e: e.tensor_tensor(out=biasL[:, :, :], in0=EBt[:, :, :], in1=biasH[:, :, :], op=ALU.subtract), reads=[EB, BH], writes=[BL])
                            load("sp", EBS, biasS[l, pq].rearrange("p (h c) -> p h c", h=2), [EBSK], ebs_ds)
                            S.op("act", lambda e: e.activation(out=EBS[:, :, :], in_=EBS[:, :, :], func=AF.Exp), reads=[EBSK], writes=[EBSK])
                            xyv = [pp[0].rearrange("p (h c) -> p h c", h=2), pp[1].rearrange("p (h c) -> p h c", h=2)]
                            zv2 = pp[2].rearrange("p (h c) -> p h c", h=2)

                            def cb_s(m):
                                nb = min(m, 4) + 1
                                k_ = m % 2
                                w4 = min(nb, 4) * 128

                                def f(e):
                                    r = None
                                    for h in range(2):
                                        for o in range(nb):
                                            if o < 4:
                                                dst = xyv[k_][:, h, o * 128:(o + 1) * 128]
                                                st = (o == 0)
                                            else:
                                                dst = zv2[:, h, k_ * 128:(k_ + 1) * 128]
                                                st = True
                                            e.matmul(dst, lhsT=kP[64 * h:64 * h + 64, (m - o) * 128:(m - o + 1) * 128], rhs=qP[64 * h:64 * h + 64, m * 128:(m + 1) * 128], start=st, stop=False, skip_group_check=True)
                                    for h in range(2):
                                        e.matmul(xyv[k_][:, h, 0:w4], lhsT=identB, rhs=biasH[:, h, 0:w4], start=False, stop=False, skip_group_check=True)
                                        r = e.matmul(xyv[k_][:, h, 0:w4], lhsT=identB, rhs=biasL[:, h, 0:w4], start=False, stop=True, skip_group_check=True)
                                        if nb == 5:
                                            e.matmul(zv2[:, h, k_ * 128:(k_ + 1) * 128], lhsT=identB, rhs=biasH[:, h, 512:640], start=False, stop=False, skip_group_check=True)
                                            r = e.matmul(zv2[:, h, k_ * 128:(k_ + 1) * 128], lhsT=identB, rhs=biasL[:, h, 512:640], start=False, stop=True, skip_group_check=True)
                                    return r
                                wr = [PB[2 * k_], PB[2 * k_ + 1]] + ([PB[4], PB[5]] if nb == 5 else [])
                                S.op("pe", f, reads=[K[m // 4], K[max(m - 4, 0) // 4], Q[m // 4], BH, BL, TC], writes=wr)
                                if nb == 5:
                                    S.op("act", lambda e: e.activation(out=pcb[k_][:, :, 512:640], in_=zv2[:, :, k_ * 128:(k_ + 1) * 128], func=AF.Exp), reads=[PB[4], PB[5]], writes=[PCB[k_]])
                                S.op("act", lambda e: e.activation(out=pcb[k_][:, :, 0:w4], in_=xyv[k_][:, :, 0:w4], func=AF.Exp), reads=[PB[2 * k_], PB[2 * k_ + 1]], writes=[PCB[k_]])

                            def cb_mul(m):
                                pass

                            def cb_av(m):
                                nb = min(m, 4) + 1
                                k_ = m % 2
                                ob = 6

                                def f2(e):
                                    r = None
                                    if os.environ.get("KAV2D"):
                                        for h in range(2):
                                            for o in range(nb):
                                                e.matmul(bank(ob)[:, h * 128:(h + 1) * 128], lhsT=vP[:, m - o, :], rhs=pcb[k_][:, h, o * 128:(o + 1) * 128], start=(o == 0), stop=(o == nb - 1))
                                        for h in range(2):
                                            for o in range(nb):
                                                r = e.matmul(bank(ob)[:, 256 + h * 128:256 + (h + 1) * 128], lhsT=ones, rhs=pcb[k_][:, h, o * 128:(o + 1) * 128], start=(o == 0), stop=(o == nb - 1))
                                        return r
                                    for o in range(nb):
                                        e.matmul(bank(ob)[:, 0:256], lhsT=vP[:, m - o, :], rhs=pcb[k_][:, :, o * 128:(o + 1) * 128], start=(o == 0), stop=(o == nb - 1))
                                    for o in range(nb):
                                        r = e.matmul(bank(ob)[:, 256:512], lhsT=ones, rhs=pcb[k_][:, :, o * 128:(o + 1) * 128], start=(o == 0), stop=(o == nb - 1))
                                    return r
                                S.op("pe", f2, reads=[PCB[k_], TC] + [V[m - o] for o in range(nb)], writes=[PB[ob]])
                                mc = (m % 4) * 128
                                od = od_l[(m // 4) % 2]
                                OSB = OSB_l[(m // 4) % 2]
                                v4 = bank(ob).rearrange("p (a b c) -> p a b c", a=2, b=2)
                                for h in range(2):
                                    S.op("dve", lambda e, h=h: e.tensor_copy(out=od[64 * h:64 * h + 64, :, mc:mc + 128], in_=v4[64 * h:64 * h + 64, :, h, :]), reads=[PB[ob]], writes=[OSB])

                            def cb_norm(g4):
                                od = od_l[g4 % 2]
                                OSB = OSB_l[g4 % 2]
                                S.op("act", lambda e: e.activation(out=od[:, 1, :], in_=od[:, 1, :], func=AF.Ln), reads=[OSB], writes=[OSB])
                                S.op("act", lambda e: e.activation(out=od[:, 1, :], in_=od[:, 1, :], func=AF.Exp, scale=-1.0), reads=[OSB], writes=[OSB])
                                S.op("dve", lambda e: e.tensor_tensor(out=od[:, 0, :], in0=od[:, 0, :], in1=od[:, 1, :], op=ALU.mult), reads=[OSB], writes=[OSB])
                                head_norm(od[:, 0, :], OSB, g4 * 512, 512, gcol_o, g4, bk=7)

                            NM = int(os.environ.get("KCBM", "16"))
                            cb_s(0)
                            cb_mul(0)
                            for m in range(NM):
                                if m + 1 < NM:
                                    cb_s(m + 1)
                                    cb_mul(m + 1)
                                cb_av(m)
                                if m % 4 == 0 and m > 0:
                                    cb_norm(m // 4 - 1)
                            cb_norm(NM // 4 - 1)
                            stop_at(6)
                            if pq == 3:
                                S.bg = wout_gen([0, 1, 2, 3])
                            load("pool", vc[:, 0:4, :], ccv[l, :, pq * 128:(pq + 1) * 128].rearrange("(k s) n -> s k n", s=128), [VC], vc_ds)
                            load("sp", kc[0], cck[l, :, pq * 128:(pq + 1) * 128].rearrange("(j s) n -> s j n", s=128), [KC[0]], kc_ds[0])
                            bk = nextbank(6, 8)

                            def f(e, bk=bk):
                                r = None
                                for j in range(4):
                                    r = e.transpose(bank(bk)[:, j * 128:(j + 1) * 128], kc[0][:, j, :], ident)
                                return r
                            S.op("pe", f, reads=[KC[0], TC], writes=[PB[bk]])
                            S.op("dve", lambda e, bk=bk: e.tensor_copy(out=kcT[:, 0:512], in_=bank(bk)[:, 0:512]), reads=[PB[bk]], writes=[KCT[0]])
                            zsv = pp[0].rearrange("p (h c) -> p h c", h=2)

                            def f(e):
                                r = None
                                for h in range(2):
                                    for kb in range(4):
                                        e.matmul(zsv[:, h, kb * 16:(kb + 1) * 16], lhsT=kcT[64 * h:64 * h + 64, kb * 128:(kb + 1) * 128], rhs=qP[64 * h:64 * h + 64, 2048:2064], start=True, stop=True)
                                    r = e.matmul(zsv[0:16, h, 64:80], lhsT=kP[64 * h:64 * h + 64, 2048:2064], rhs=qP[64 * h:64 * h + 64, 2048:2064], start=True, stop=True)
                                return r
                            S.op("pe", f, reads=[KCT[0], K[4], Q[4]], writes=[PB[0], PB[1]])
                            S.op("act", lambda e: e.activation(out=ecs[:, :, 0:64], in_=zsv[:, :, 0:64], func=AF.Exp), reads=[PB[0], PB[1]], writes=[ECS])
                            S.op("act", lambda e: e.activation(out=ecs[0:16, :, 64:80], in_=zsv[0:16, :, 64:80], func=AF.Exp), reads=[PB[0], PB[1]], writes=[ECS])
                            S.op("dve", lambda e: e.tensor_tensor(out=pcs[:, :, 0:64], in0=ecs[:, :, 0:64], in1=EBS[:, :, 0:64], op=ALU.mult), reads=[ECS, EBSK], writes=[PCS])
                            S.op("dve", lambda e: e.tensor_tensor(out=pcs[0:16, :, 64:80], in0=ecs[0:16, :, 64:80], in1=EBS[0:16, :, 64:80], op=ALU.mult), reads=[ECS, EBSK], writes=[PCS])

                            def f(e):
                                r = None
                                for h in range(2):
                                    o = pp[3][:, h * 512:h * 512 + 16]
                                    d = pp[3][:, h * 512 + 16:h * 512 + 32]
                                    for kb in range(4):
                                        e.matmul(o, lhsT=vc[:, kb, :], rhs=pcs[:, h, kb * 16:(kb + 1) * 16], start=(kb == 0), stop=False)
                                    e.matmul(o, lhsT=vP[0:16, 16, :], rhs=pcs[0:16, h, 64:80], start=False, stop=True)
                                    for kb in range(4):
                                        e.matmul(d, lhsT=ones, rhs=pcs[:, h, kb * 16:(kb + 1) * 16], start=(kb == 0), stop=False)
                                    r = e.matmul(d, lhsT=ones[0:16, :], rhs=pcs[0:16, h, 64:80], start=False, stop=True)
                                return r
                            S.op("pe", f, reads=[VC, V[16], PCS, TC], writes=[PB[6], PB[7]])
                            for h in range(2):
                                S.op("dve", lambda e, h=h: e.reciprocal(out=recs[64 * h:64 * h + 64, :], in_=pp[3][64 * h:64 * h + 64, h * 512 + 16:h * 512 + 32]), reads=[PB[6], PB[7]], writes=[RECS])
                            for h in range(2):
                                S.op("dve", lambda e, h=h: e.tensor_tensor(out=oss[64 * h:64 * h + 64, 0:16], in0=pp[3][64 * h:64 * h + 64, h * 512:h * 512 + 16], in1=recs[64 * h:64 * h + 64, :], op=ALU.mult), reads=[PB[6], PB[7], RECS], writes=[OSS])
                            head_norm(oss, OSS, 2048, 16, gcol_o, 4)

                        stop_at(4)
                        if g == 1:
                            stop_at(7)
                        if g == 1 and pq < 3:
                            wg_ = wout_gen([0, 1, 2, 3])
                            while S.tick():
                                next(wg_, None)
                            for _ in wg_:
                                pass
                        S.drain_bg()
                        for _ in wout_gen([4]):
                            pass
                        if pq + 2 < 4:
                            issue_loads(pq + 2)

                stop_at(8)
                arB.reset()
                arF.reset()
                norm_to_h(g0 + 8)
                NS = 2
                ring = []
                RG = []
                for s_ in range(NS):
                    ap, tk = arB.alloc(12288)
                    ring.append(ap)
                    RG.append(tk)
                actT = []
                ACTT = []
                for b in range(2):
                    ap, tk = arB.alloc(4 * 512, 4)
                    actT.append(ap.rearrange("p (f t) -> p f t", f=4))
                    ACTT.append(tk)
                tmp = []
                TMP = []
                for b in range(2):
                    ap, tk = arF.alloc(512)
                    tmp.append(ap)
                    TMP.append(tk)
                pT_f, PT = arB.alloc(2 * T, 5)
                pT = pT_f.rearrange("p (c t) -> p c t", c=2)
                pst = []
                PST = []
                for b in range(2):
                    ap, tk = arF.alloc(256)
                    pst.append(ap)
                    PST.append(tk)
                sig = []
                SIG = []
                for b in range(2):
                    ap, tk = arF.alloc(512)
                    sig.append(ap)
                    SIG.append(tk)

                def p_gen():
                    for i, (t0, tn) in enumerate(TOK128):
                        b = i % 2
                        tt = min(t0 // 512, 4)
                        load("sp", pst[b][0:tn, :], pin[l, t0:t0 + tn, :], [PST[b]], pst_ds[b])
                        bk = 7

                        def f(e):
                            r = None
                            for c in range(2):
                                r = e.transpose(bank(bk)[:, c * 128:c * 128 + tn], pst[b][0:tn, c * 128:(c + 1) * 128], ident[0:tn, 0:tn])
                            return r
                        S.op("pe", f, reads=[PST[b], TC], writes=[PB[bk]])
                        S.op("dve", lambda e: e.tensor_copy(out=pT[:, :, t0:t0 + tn], in_=bank(bk)[:, 0:256].rearrange("p (c t) -> p c t", c=2)[:, :, 0:tn]), reads=[PB[bk]], writes=[PT[tt]])
                        yield

                groups = [(0, 4), (4, 4), (8, 4), (12, 4), (16, 4), (20, 2)]

                def ffn_load(gi):
                    f0, G = groups[gi]
                    s_ = gi % NS
                    gv = ring[s_][:, 0:4096].rearrange("p (c n) -> p c n", c=8)
                    uv = ring[s_][:, 4096:8192].rearrange("p (c n) -> p c n", c=8)
                    dv = ring[s_][:, 8192:12288].rearrange("p (f n) -> p f n", f=4)
                    load("pool", gv[:, :, 0:G * 128], w_gate[l, :, f0 * 128:(f0 + G) * 128].rearrange("(c p) n -> p c n", p=128), [RG[s_]], ring_ds[s_][0])
                    load("pool", uv[:, :, 0:G * 128], w_up[l, :, f0 * 128:(f0 + G) * 128].rearrange("(c p) n -> p c n", p=128), [RG[s_]], ring_ds[s_][1])
                    load("pool", dv[:, 0:G, :], w_down[l, f0 * 128:(f0 + G) * 128, :].rearrange("(f p) n -> p f n", p=128), [RG[s_]], ring_ds[s_][2])

                ffn_load(0)
                ti = [0]

                def views(gi):
                    s_ = gi % NS
                    return (s_, ring[s_][:, 0:4096].rearrange("p (c n) -> p c n", c=8),
                            ring[s_][:, 4096:8192].rearrange("p (c n) -> p c n", c=8),
                            ring[s_][:, 8192:12288].rearrange("p (f n) -> p f n", f=4))

                def after_group_start(gi):
                    nonlocal_box = None
                    if gi + 1 < len(groups):
                        ffn_load(gi + 1)
                    else:
                        load("pool", wg, w_pg[l].rearrange("(c p) n -> p c n", p=128), [WG], wg_ds)
                        load("pool", wp, w_pp[l].rearrange("(c p) n -> p c n", p=128), [WP], wp_ds)
                        S.bg2 = p_gen()

                wg = ring[0][:, 0:8192].rearrange("p (c n) -> p c n", c=8)
                wp = ring[0][:, 8192:10240].rearrange("p (c n) -> p c n", c=2)
                WG = RG[0]
                WP = RG[0]

                def gu(gi, tt, ab):
                    f0, G = groups[gi]
                    s_, gv, uv, dv = views(gi)
                    t0, tn = TT[tt]
                    for fi in range(G):
                        bg = nextbank(0, 4)
                        bu = nextbank(0, 4)

                        def fg(e):
                            r = None
                            for c in range(8):
                                r = e.matmul(bank(bg)[:, 0:tn], lhsT=gv[:, c, fi * 128:(fi + 1) * 128], rhs=hT[:, c, t0:t0 + tn], start=(c == 0), stop=(c == 7))
                            return r
                        S.op("pe", fg, reads=[RG[s_]] + [H[c][tt] for c in range(8)], writes=[PB[bg]])

                        def fu(e):
                            r = None
                            for c in range(8):
                                r = e.matmul(bank(bu)[:, 0:tn], lhsT=uv[:, c, fi * 128:(fi + 1) * 128], rhs=hT[:, c, t0:t0 + tn], start=(c == 0), stop=(c == 7))
                            return r
                        S.op("pe", fu, reads=[RG[s_]] + [H[c][tt] for c in range(8)], writes=[PB[bu]])
                        tb = ti[0] % 2
                        ti[0] += 1
                        S.op("act", lambda e: e.activation(out=tmp[tb][:, 0:tn], in_=bank(bg)[:, 0:tn], func=AF.Silu), reads=[PB[bg]], writes=[TMP[tb]])
                        S.op("dve", lambda e: e.tensor_tensor(out=actT[ab][:, fi, 0:tn], in0=tmp[tb][:, 0:tn], in1=bank(bu)[:, 0:tn], op=ALU.mult), reads=[PB[bu], TMP[tb]], writes=[ACTT[ab][fi]])

                def dn(gi, tt, ab):
                    f0, G = groups[gi]
                    s_, gv, uv, dv = views(gi)
                    t0, tn = TT[tt]
                    for dc in range(8):
                        bd = nextbank(4, 8)

                        def fd(e):
                            r = None
                            for fi in range(G):
                                r = e.matmul(bank(bd)[:, 0:tn], lhsT=dv[:, fi, dc * 128:(dc + 1) * 128], rhs=actT[ab][:, fi, 0:tn], start=(fi == 0), stop=(fi == G - 1))
                            return r
                        S.op("pe", fd, reads=[RG[s_]] + [ACTT[ab][fi] for fi in range(G)], writes=[PB[bd]])
                        add_to_x(bd, dc, tt)
                        if dc % 2 == 1:
                            S.tick()

                seq = [(gi, tt) for gi in range(len(groups)) for tt in range(len(TT))]
                gu(seq[0][0], seq[0][1], 0)
                after_group_start(0)
                for k in range(len(seq)):
                    gi, tt = seq[k]
                    if k + 1 < len(seq):
                        gu(seq[k + 1][0], seq[k + 1][1], (k + 1) % 2)
                    dn(gi, tt, k % 2)
                    if k + 1 < len(seq) and seq[k + 1][1] == 0:
                        after_group_start(seq[k + 1][0])

                stop_at(9)
                S.drain_bg2()
                norm_to_h(g0 + 16)
                si = 0
                for tt, (t0, tn) in enumerate(TT):
                    for dc in range(8):
                        bgk = nextbank()
                        bpk = nextbank()

                        def fg(e, bk=bgk, dc=dc, t0=t0, tn=tn):
                            r = None
                            for c in range(8):
                                r = e.matmul(bank(bk)[:, 0:tn], lhsT=wg[:, c, dc * 128:(dc + 1) * 128], rhs=hT[:, c, t0:t0 + tn], start=(c == 0), stop=(c == 7))
                            return r
                        S.op("pe", fg, reads=[WG] + [H[c][tt] for c in range(8)], writes=[PB[bgk]])

                        def fp(e, bk=bpk, dc=dc, t0=t0, tn=tn):
                            r = None
                            for c in range(2):
                                r = e.matmul(bank(bk)[:, 0:tn], lhsT=wp[:, c, dc * 128:(dc + 1) * 128], rhs=pT[:, c, t0:t0 + tn], start=(c == 0), stop=(c == 1))
                            return r
                        S.op("pe", fp, reads=[WP, PT[tt]], writes=[PB[bpk]])
                        sb_ = si % 2
                        si += 1
                        S.op("act", lambda e, sb_=sb_, bk=bgk, tn=tn: e.activation(out=sig[sb_][:, 0:tn], in_=bank(bk)[:, 0:tn], func=AF.Sigmoid), reads=[PB[bgk]], writes=[SIG[sb_]])
                        S.op("dve", lambda e, sb_=sb_, bk=bpk, tn=tn: e.tensor_tensor(out=sig[sb_][:, 0:tn], in0=sig[sb_][:, 0:tn], in1=bank(bk)[:, 0:tn], op=ALU.mult), reads=[PB[bpk], SIG[sb_]], writes=[SIG[sb_]])
                        S.op("pool", lambda e, sb_=sb_, dc=dc, t0=t0, tn=tn: e.tensor_tensor(out=xT[:, dc, t0:t0 + tn], in0=xT[:, dc, t0:t0 + tn], in1=sig[sb_][:, 0:tn], op=ALU.add), reads=[SIG[sb_], X[dc][tt]], writes=[X[dc][tt]])

            arF.reset()
            yst_f, _yt = arF.alloc(4096, 4)
            yst_t = yst_f.rearrange("p (b n) -> p b n", b=4)
            YST = list(_yt)
            yst_ds = xst_ds

            def out_tiles(tt):
                idx = [4 * tt + j for j in range(4)] if tt < 4 else [16]
                for i in idx:
                    t0, tn = TOK128[i]
                    b = i % 4
                    for half in range(2):
                        bk = nextbank()

                        def f(e):
                            r = None
                            for j in range(4):
                                c = half * 4 + j
                                r = e.transpose(bank(bk)[0:tn, j * 128:(j + 1) * 128], xT[:, c, t0:t0 + tn], ident)
                            return r
                        S.op("pe", f, reads=[X[c][tt] for c in range(half * 4, half * 4 + 4)] + [TC], writes=[PB[bk]])
                        if half == 0:
                            S.op("dve", lambda e: e.tensor_copy(out=yst_t[0:tn, b, 0:512], in_=bank(bk)[0:tn, 0:512]), reads=[PB[bk]], writes=[YST[b]])
                        else:
                            S.op("act", lambda e: e.activation(out=yst_t[0:tn, b, 512:1024], in_=bank(bk)[0:tn, 0:512], func=AF.Copy), reads=[PB[bk]], writes=[YST[b]])
                    store(y[t0:t0 + tn, :], yst_t[0:tn, b, :], [YST[b]], yst_ds[b])

            norm_to_h(64, out_fp32=True, after=out_tiles)

        except StopBuild:
            pass
        S.final_wait("sp", out_handles)
        S.run()
    return nc


_NC = [None]


def _consts():
    cf = np.zeros((128, NCF), np.float32)
    cf[:, 0:128] = np.eye(128, dtype=np.float32)
    sl = np.arange(128)[:, None]
    mk = np.zeros((128, 5, 128), np.float32)
    tl = np.arange(128)[None, :]
    mk[:, 0, :] = np.where((sl >= 64) & (tl < 64), -30000.0, 0.0)
    mk[:, 4, :] = np.where((sl < 64) & (tl >= 64), -30000.0, 0.0)
    cf[:, 128:768] = mk.reshape(128, 640)
    cb = np.zeros((128, NCB), np.float32)
    j = np.arange(128)[:, None]
    s = np.arange(128)[None, :]
    cb[:, 0:128] = np.where(j >= s, -1.0, 0.0)
    cb[:, 128:256] = -1.0
    cb[:, 256:384] = 1.0
    cb[:, 384:512] = np.where((j // 64) == (s // 64), 1.0, 0.0)
    t5 = np.arange(512)[None, :]
    for r in range(4):
        cb[:, 512 + r * 512:512 + (r + 1) * 512] = np.where(128 * r + sl < t5, 1.0, 0.0)
    cb[0:16, 2560:2576] = np.where(np.arange(16)[:, None] < np.arange(16)[None, :], 1.0, 0.0)
    cb[:, 2576:2704] = np.eye(128, dtype=np.float32)
    cb[:, 2704:2832] = np.where(j >= s, -30000.0, 0.0)
    return cf, cb


def kernel(x_prompt, x_sample, p_prompt, p_sample, cache_sb_k, cache_sb_v, cache_cb_k, cache_cb_v,
           g_mix, w_in, rel_table, g_out_sb, g_out_cb, w_out, g_ffn, w_gate, w_up, w_down,
           g_ple, w_ple_gate, w_ple_proj, g_final):
    f = lambda a: np.ascontiguousarray(np.asarray(a, dtype=np.float32))
    x_prompt, x_sample, p_prompt, p_sample = f(x_prompt), f(x_sample), f(p_prompt), f(p_sample)
    rel_table = f(rel_table)
    cf, cb = _consts()
    gpk = np.zeros((128, 72), np.float32)
    for l in range(2):
        gpk[:, 32 * l + 0:32 * l + 8] = f(g_mix)[l].reshape(8, 128).T
        gpk[:, 32 * l + 8:32 * l + 16] = f(g_ffn)[l].reshape(8, 128).T
        gpk[:, 32 * l + 16:32 * l + 24] = f(g_ple)[l].reshape(8, 128).T
        gpk[:, 32 * l + 24:32 * l + 28] = f(g_out_sb)[l].reshape(4, 128).T
        gpk[:, 32 * l + 28:32 * l + 32] = f(g_out_cb)[l].reshape(4, 128).T
    gpk[:, 64:72] = f(g_final).reshape(8, 128).T
    sl = np.arange(128)[:, None, None]
    o = np.arange(5)[None, :, None]
    tl = np.arange(128)[None, None, :]
    idxP = np.clip(128 * o + tl - sl, -128, 128) + 128
    bP = rel_table[:, idxP, :]
    bP = np.ascontiguousarray(np.transpose(bP, (0, 4, 1, 2, 3))).reshape(2, 8, 128, 640)
    r = np.arange(128)[:, None, None]
    blk = np.arange(4)[None, :, None]
    i16 = np.arange(16)[None, None, :]
    idxS = np.clip(512 - (blk * 128 + r) + i16, -128, 128) + 128
    bS = np.zeros((2, 8, 128, 80), np.float32)
    bS[:, :, :, 0:64] = np.transpose(rel_table[:, idxS, :], (0, 4, 1, 2, 3)).reshape(2, 8, 128, 64)
    jn = np.arange(16)[:, None]
    idxN = np.clip(np.arange(16)[None, :] - jn, -128, 128) + 128
    bS[:, :, 0:16, 64:80] = np.transpose(rel_table[:, idxN, :], (0, 3, 1, 2))
    bS = np.ascontiguousarray(bS.reshape(2, 4, 2, 128, 80).transpose(0, 1, 3, 2, 4)).reshape(2, 4, 128, 160)

    shared = {
        "w_in": f(w_in), "w_out": f(w_out), "w_gate": f(w_gate), "w_up": f(w_up), "w_down": f(w_down),
        "w_pg": f(w_ple_gate), "w_pp": f(w_ple_proj), "biasP": bP, "biasS": bS, "gpack": gpk,
        "constF": cf, "constB": cb,
    }
    csk, csv, cck, ccv = f(cache_sb_k), f(cache_sb_v), f(cache_cb_k), f(cache_cb_v)
    in_maps = []
    for c in range(8):
        m = dict(shared)
        m["xin"] = np.ascontiguousarray(np.concatenate([x_prompt[c], x_sample[c]], axis=0))
        m["pin"] = np.ascontiguousarray(np.concatenate([p_prompt[:, c], p_sample[:, c]], axis=1))
        m["csk"] = np.ascontiguousarray(csk[:, c].reshape(2, 2048, 512))
        m["csv"] = np.ascontiguousarray(csv[:, c].reshape(2, 2048, 512))
        m["cck"] = np.ascontiguousarray(cck[:, c].reshape(2, 512, 512))
        m["ccv"] = np.ascontiguousarray(ccv[:, c].reshape(2, 512, 512))
        in_maps.append(m)
    if _NC[0] is None:
        _NC[0] = build()
    res = run_bass_kernel_spmd(_NC[0], in_maps, core_ids=list(range(8)))
    R = res.results
    yo = np.stack([R[c]["y"] for c in range(8)])
    sbko = np.stack([R[c]["sbk"] for c in range(8)], axis=1)
    sbvo = np.stack([R[c]["sbv"] for c in range(8)], axis=1)
    cbko = np.stack([R[c]["cbk"] for c in range(8)], axis=1)
    cbvo = np.stack([R[c]["cbv"] for c in range(8)], axis=1)
    h4 = lambda a: np.ascontiguousarray(a).reshape(a.shape[0], a.shape[1], a.shape[2], 8, 64)
    return (np.ascontiguousarray(yo[:, :2048]), np.ascontiguousarray(yo[:, 2048:]),
            h4(sbko[:, :, :2048]), h4(sbvo[:, :, :2048]),
            h4(cbko[:, :, :512]), h4(cbvo[:, :, :512]),
            h4(sbko[:, :, 2048:]), h4(sbvo[:, :, 2048:]),
            h4(cbko[:, :, 512:]), h4(cbvo[:, :, 512:]))
```
